# Optimizing a Trainium2 kernel written in Bass

```python
import math
import jax, jax.numpy as jnp
from jax import lax
import numpy as np

D_MODEL = 1024
BATCH = 8
SEQ = 2048
DEPTH = 1

CHUNK = 64
QBLOCK = 128

ATT_HEADS = 8
ATT_HEAD_DIM = 64
ATT_V_DIM = 2 * ATT_HEAD_DIM
ATT_WIDTH = ATT_HEADS * ATT_V_DIM

SSM_HEADS = 16
SSM_HEAD_DIM = 64
SSM_WIDTH = SSM_HEADS * SSM_HEAD_DIM
SSM_GROUPS = 2
SSM_STATE = 128
SSM_CONV = 4
SSM_HEADS_PER_GROUP = SSM_HEADS // SSM_GROUPS

MIX_WIDTH = ATT_WIDTH + SSM_WIDTH

FFN_DIM = 2816
FFN_CONV = 3

REL_BUCKETS = 32
REL_MAX_DIST = 128

NORM_EPS = 1e-6
SUBLN_EPS = 1e-5
SSM_NORM_EPS = 1e-5

Q_COLS = ATT_HEADS * 2 * ATT_HEAD_DIM
K_COLS = ATT_HEADS * 2 * ATT_HEAD_DIM
V_COLS = ATT_WIDTH
Z_COLS = SSM_WIDTH
XBC_COLS = SSM_WIDTH + 2 * SSM_GROUPS * SSM_STATE
DT_COLS = SSM_HEADS
IN_COLS = Q_COLS + K_COLS + V_COLS + Z_COLS + XBC_COLS + DT_COLS
IN_SPLITS = (Q_COLS, Q_COLS + K_COLS, Q_COLS + K_COLS + V_COLS,
             Q_COLS + K_COLS + V_COLS + Z_COLS,
             Q_COLS + K_COLS + V_COLS + Z_COLS + XBC_COLS)

kernel_name = "hymba_diffattn_ssd_convffn_block"


def rms_norm(x, g, eps=NORM_EPS):
    x32 = x.astype(jnp.float32)
    y = x32 * lax.rsqrt(jnp.mean(x32 * x32, axis=-1, keepdims=True) + eps)
    return (y * g.astype(jnp.float32)).astype(x.dtype)


def causal_depthwise_conv(x, w, b):
    k = w.shape[0]
    out = lax.conv_general_dilated(
        x, w[:, None, :].astype(x.dtype), window_strides=(1,),
        padding=[(k - 1, 0)], dimension_numbers=('NWC', 'WIO', 'NWC'),
        feature_group_count=x.shape[-1])
    return out + b.astype(x.dtype)


def t5_bucket(rel):
    nb = REL_BUCKETS // 2
    max_exact = nb // 2
    bucket = jnp.where(rel > 0, nb, 0)
    n = jnp.abs(rel)
    nf = jnp.maximum(n, 1).astype(jnp.float32)
    large = max_exact + (jnp.log(nf / max_exact) / math.log(REL_MAX_DIST / max_exact)
                         * (nb - max_exact)).astype(jnp.int32)
    large = jnp.minimum(large, nb - 1)
    return bucket + jnp.where(n < max_exact, n, large)


def differential_attention(q, k, v, rel_table, lam, sub_g, lam_init):
    b, seq_len = q.shape[0], q.shape[1]
    scale = ATT_HEAD_DIM ** -0.5
    outs = []
    for blk in range(seq_len // QBLOCK):
        q0 = blk * QBLOCK
        kv_len = q0 + QBLOCK
        qb = q[:, q0:kv_len]
        kb = k[:, :kv_len]
        vb = v[:, :kv_len]
        q_pos = jnp.arange(q0, kv_len)
        k_pos = jnp.arange(kv_len)
        bias = jnp.transpose(rel_table[t5_bucket(k_pos[None, :] - q_pos[:, None])], (2, 0, 1))
        s = jnp.einsum('bqhmd,bkhmd->bhmqk', qb, kb).astype(jnp.float32) * scale
        s = s + bias[None, :, None].astype(jnp.float32)
        allowed = (k_pos[None, :] // CHUNK) <= (q_pos[:, None] // CHUNK)
        s = jnp.where(allowed, s, -1e30)
        p = jax.nn.softmax(s, axis=-1)
        a = p[:, :, 0] - lam * p[:, :, 1]
        outs.append(jnp.einsum('bhqk,bkhe->bqhe', a.astype(v.dtype), vb))
    o = jnp.concatenate(outs, axis=1)
    o = rms_norm(o, sub_g, SUBLN_EPS) * (1.0 - lam_init)
    return o.reshape(b, seq_len, ATT_WIDTH)


def ssd_mixer(z, xbc, dt_raw, conv_w, conv_b, dt_bias, a_log, d_skip, norm_g):
    b, seq_len, _ = xbc.shape
    G, R, P, N = SSM_GROUPS, SSM_HEADS_PER_GROUP, SSM_HEAD_DIM, SSM_STATE
    nc = seq_len // CHUNK
    f32 = jnp.float32
    xbc = jax.nn.silu(causal_depthwise_conv(xbc, conv_w, conv_b))
    xs, bm, cm = jnp.split(xbc, (SSM_WIDTH, SSM_WIDTH + G * N), axis=-1)
    xs = xs.astype(f32).reshape(b, nc, CHUNK, G, R, P)
    bm = bm.astype(f32).reshape(b, nc, CHUNK, G, N)
    cm = cm.astype(f32).reshape(b, nc, CHUNK, G, N)
    dt = jax.nn.softplus(dt_raw.astype(f32) + dt_bias.astype(f32)).reshape(b, nc, CHUNK, G, R)
    a = -jnp.exp(a_log.astype(f32)).reshape(G, R) * dt
    xdt = xs * dt[..., None]
    a_cum = jnp.cumsum(a, axis=2)
    seg = a_cum[:, :, :, None] - a_cum[:, :, None, :]
    tri = jnp.tril(jnp.ones((CHUNK, CHUNK), dtype=bool))[:, :, None, None]
    decay = jnp.exp(jnp.where(tri, seg, -jnp.inf))
    cb = jnp.einsum('bclgn,bcsgn->bclsg', cm, bm)
    y_diag = jnp.einsum('bclsg,bclsgr,bcsgrp->bclgrp', cb, decay, xdt)
    decay_to_end = jnp.exp(a_cum[:, :, -1:] - a_cum)
    chunk_states = jnp.einsum('bclgn,bclgr,bclgrp->bcgrpn', bm, decay_to_end, xdt)
    chunk_decay = jnp.exp(a_cum[:, :, -1])

    def step(h, inp):
        s_c, d_c = inp
        return d_c[..., None, None] * h + s_c, h

    h0 = jnp.zeros((b, G, R, P, N), f32)
    _, prev = lax.scan(step, h0, (jnp.moveaxis(chunk_states, 1, 0), jnp.moveaxis(chunk_decay, 1, 0)))
    prev = jnp.moveaxis(prev, 0, 1)
    y_off = jnp.einsum('bclgn,bcgrpn,bclgr->bclgrp', cm, prev, jnp.exp(a_cum))
    y = y_diag + y_off + xs * d_skip.astype(f32).reshape(G, R)[..., None]
    y = y.reshape(b, seq_len, SSM_WIDTH)
    g = (y * jax.nn.silu(z.astype(f32))).reshape(b, seq_len, G, SSM_WIDTH // G)
    g = g * lax.rsqrt(jnp.mean(g * g, axis=-1, keepdims=True) + SSM_NORM_EPS)
    return (g.reshape(b, seq_len, SSM_WIDTH) * norm_g.astype(f32)).astype(z.dtype)


def conv_ffn(h, w_up, conv_w, conv_b, w_down):
    u = causal_depthwise_conv(h @ w_up, conv_w, conv_b)
    gate, val = jnp.split(u, 2, axis=-1)
    return (jax.nn.silu(gate) * val) @ w_down


def setup_inputs(seed: int = 0) -> dict:
    key = jax.random.key(seed)
    ks = jax.random.split(key, 24)
    f32 = jnp.float32
    nrm = lambda k, shape, s: jax.random.normal(k, shape, f32) * s
    dt0 = jnp.exp(jax.random.uniform(ks[10], (DEPTH, SSM_HEADS), f32, math.log(1e-3), math.log(1e-1)))
    return {
        "x": nrm(ks[0], (BATCH, SEQ, D_MODEL), 1.0),
        "rel_bias_table": nrm(ks[1], (REL_BUCKETS, ATT_HEADS), 0.1),
        "attn_norm_g": 1.0 + nrm(ks[2], (DEPTH, D_MODEL), 0.02),
        "w_in": nrm(ks[3], (DEPTH, D_MODEL, IN_COLS), D_MODEL ** -0.5),
        "lambda_q1": nrm(ks[4], (DEPTH, ATT_HEAD_DIM), 0.1),
        "lambda_k1": nrm(ks[5], (DEPTH, ATT_HEAD_DIM), 0.1),
        "lambda_q2": nrm(ks[6], (DEPTH, ATT_HEAD_DIM), 0.1),
        "lambda_k2": nrm(ks[7], (DEPTH, ATT_HEAD_DIM), 0.1),
        "attn_subln_g": 1.0 + nrm(ks[8], (DEPTH, ATT_V_DIM), 0.02),
        "ssm_conv_w": nrm(ks[9], (DEPTH, SSM_CONV, XBC_COLS), SSM_CONV ** -0.5),
        "ssm_conv_b": nrm(ks[11], (DEPTH, XBC_COLS), 0.02),
        "ssm_dt_bias": dt0 + jnp.log(-jnp.expm1(-dt0)),
        "ssm_a_log": jnp.log(jax.random.uniform(ks[12], (DEPTH, SSM_HEADS), f32, 1.0, 16.0)),
        "ssm_d": 1.0 + nrm(ks[13], (DEPTH, SSM_HEADS), 0.1),
        "ssm_norm_g": 1.0 + nrm(ks[14], (DEPTH, SSM_WIDTH), 0.02),
        "w_out": nrm(ks[15], (DEPTH, MIX_WIDTH, D_MODEL), MIX_WIDTH ** -0.5),
        "ffn_norm_g": 1.0 + nrm(ks[16], (DEPTH, D_MODEL), 0.02),
        "ffn_w_up": nrm(ks[17], (DEPTH, D_MODEL, 2 * FFN_DIM), D_MODEL ** -0.5),
        "ffn_conv_w": nrm(ks[18], (DEPTH, FFN_CONV, 2 * FFN_DIM), FFN_CONV ** -0.5),
        "ffn_conv_b": nrm(ks[19], (DEPTH, 2 * FFN_DIM), 0.02),
        "ffn_w_down": nrm(ks[20], (DEPTH, FFN_DIM, D_MODEL), FFN_DIM ** -0.5),
        "final_norm_g": 1.0 + nrm(ks[21], (D_MODEL,), 0.02),
    }


def reference(x, rel_bias_table, attn_norm_g, w_in, lambda_q1, lambda_k1, lambda_q2, lambda_k2,
              attn_subln_g, ssm_conv_w, ssm_conv_b, ssm_dt_bias, ssm_a_log, ssm_d, ssm_norm_g,
              w_out, ffn_norm_g, ffn_w_up, ffn_conv_w, ffn_conv_b, ffn_w_down, final_norm_g):
    b, seq_len, _ = x.shape
    for i in range(DEPTH):
        lam_init = 0.8 - 0.6 * math.exp(-0.3 * i)
        h = rms_norm(x, attn_norm_g[i])
        proj = h @ w_in[i]
        q, k, v, z, xbc, dt_raw = jnp.split(proj, IN_SPLITS, axis=-1)
        q = q.reshape(b, seq_len, ATT_HEADS, 2, ATT_HEAD_DIM)
        k = k.reshape(b, seq_len, ATT_HEADS, 2, ATT_HEAD_DIM)
        v = v.reshape(b, seq_len, ATT_HEADS, ATT_V_DIM)
        lam = (jnp.exp(jnp.sum(lambda_q1[i].astype(jnp.float32) * lambda_k1[i].astype(jnp.float32)))
               - jnp.exp(jnp.sum(lambda_q2[i].astype(jnp.float32) * lambda_k2[i].astype(jnp.float32)))
               + lam_init)
        att_out = differential_attention(q, k, v, rel_bias_table, lam, attn_subln_g[i], lam_init)
        ssm_out = ssd_mixer(z, xbc, dt_raw, ssm_conv_w[i], ssm_conv_b[i], ssm_dt_bias[i],
                            ssm_a_log[i], ssm_d[i], ssm_norm_g[i])
        x = x + jnp.concatenate([att_out, ssm_out], axis=-1) @ w_out[i]
        h = rms_norm(x, ffn_norm_g[i])
        x = x + conv_ffn(h, ffn_w_up[i], ffn_conv_w[i], ffn_conv_b[i], ffn_w_down[i])
    return rms_norm(x, final_norm_g)
```

```python
import math
from contextlib import ExitStack

import numpy as np
import concourse.bass as bass
import concourse.mybir as mybir
from concourse.bass_utils import run_bass_kernel_spmd

F32 = mybir.dt.float32
BF16 = mybir.dt.bfloat16
AF = mybir.ActivationFunctionType
ALU = mybir.AluOpType

D = 1024
T = 2048
NB = 16
IN_COLS = 5648
Q0, K0, V0, Z0, XBC0, DT0 = 0, 1024, 2048, 3072, 4096, 5632
FFN = 2816
EPS = 1e-6
SUB_EPS = 1e-5
SSM_EPS = 1e-5
LAM_INIT = 0.8 - 0.6 * math.exp(-0.3 * 0)

C_GA, C_GF, C_SUBG, C_LAM = 0, 8, 16, 17
C_CWS = 273
C_CBS = 321
C_CWF = 333
C_CBF = 465
C_DTB = 509
C_ALOG = 525
C_D = 541
C_FAR = 557
C_FG = 565
C_NG = 1589
C_ID = 2613
C_U = 2741
C_L = 2869
C_ONE = 2997
C_DCOL = 3125
C_NGCOL = 3133
NCP = 3144

WORK_WORDS = 16640


class Res:
    __slots__ = ("name", "w", "r", "dsem", "dcnt", "excl")

    def __init__(self, name, excl=False):
        self.name = name
        self.excl = excl
        self.w = None
        self.r = {}
        self.dsem = None
        self.dcnt = 0


class KB:
    def __init__(self, nc, es):
        self.nc = nc
        self.es = es
        self.eng = dict(pe=nc.tensor, act=nc.scalar, dve=nc.vector, pool=nc.gpsimd, sp=nc.sync)
        self.sem = {k: es.enter_context(nc.semaphore("c_" + k)) for k in ("pe", "act", "dve", "pool")}
        self.cnt = {k: 0 for k in self.sem}
        self.seen = {k: {} for k in self.eng}
        self.nsem = 0
        self.dsems = {}

    def _collect(self, reads, writes, e=None):
        deps = {}

        def add(kind, src, val):
            key = (kind, src)
            if deps.get(key, 0) < val:
                deps[key] = val

        for r in reads:
            if r.w is not None:
                add(*r.w)
            if r.excl:
                for (kind, src), val in r.r.items():
                    if not (kind == "e" and src == e):
                        add(kind, src, val)
        for w in writes:
            if w.w is not None:
                add(*w.w)
            for (kind, src), val in w.r.items():
                add(kind, src, val)
        return deps

    def _wait(self, e, deps):
        for (kind, src), val in deps.items():
            if kind == "e" and src == "pe" and e == "pe":
                continue
            if self.seen[e].get((kind, src), 0) >= val:
                continue
            sem = self.sem[src] if kind == "e" else self.dsems[src]
            self.eng[e].wait_ge(sem, val)
            self.seen[e][(kind, src)] = val

    def op(self, e, fn, reads=(), writes=()):
        self._wait(e, self._collect(reads, writes, e))
        ins = fn(self.eng[e])
        self.cnt[e] += 1
        ins.then_inc(self.sem[e], 1)
        seq = self.cnt[e]
        for r in reads:
            r.r[("e", e)] = seq
        for w in writes:
            w.w = ("e", e, seq)
            w.r = {}
        return ins

    def _dsem(self, r):
        if r.dsem is None:
            name = "d%d" % self.nsem
            self.nsem += 1
            r.dsem = name
            self.dsems[name] = self.es.enter_context(self.nc.semaphore(name))
        return r.dsem

    def dma(self, q, out, in_, reads=(), writes=(), **kw):
        self._wait(q, self._collect(reads, writes))
        ins = self.eng[q].dma_start(out=out, in_=in_, **kw)
        own = writes[0] if writes else reads[0]
        name = self._dsem(own)
        own.dcnt += 16
        ins.then_inc(self.dsems[name], 16)
        for r in reads:
            r.r[("d", name)] = own.dcnt
        for w in writes:
            w.w = ("d", name, own.dcnt)
            w.r = {}
        return ins

    def barrier(self, res_list=()):
        deps = {}
        for k, v in self.cnt.items():
            if v > 0:
                deps[("e", k)] = v
        for r in res_list:
            if r.dsem is not None and r.dcnt > 0:
                deps[("d", r.dsem)] = r.dcnt
        for e in self.eng:
            d = {k: v for k, v in deps.items() if not (k == ("e", "pe") and e == "pe")}
            if e == "pe" and self.cnt["pe"] > 0 and self.seen["pe"].get(("e", "pe"), 0) < self.cnt["pe"]:
                self.eng["pe"].wait_ge(self.sem["pe"], self.cnt["pe"])
                self.seen["pe"][("e", "pe")] = self.cnt["pe"]
            self._wait(e, d)


class Sched:
    LAT = 0.15

    def __init__(self, k, window=160):
        self.k = k
        self.ops = []
        self.lastw = {}
        self.readers = {}
        self.window = window

    DEF_DUR = dict(pe=0.15, act=0.7, dve=0.65, pool=1.5)
    def_scale = 1.0

    def op(self, e, fn, reads=(), writes=(), dur=None, tbl=None):
        if dur is None:
            dur = self.DEF_DUR[e]
        i = len(self.ops)
        preds = set()
        for r in reads:
            if r in self.lastw:
                preds.add(self.lastw[r])
            if r.excl:
                preds |= {p for p in self.readers.get(r, ()) if self.ops[p][0] != e}
        for w in writes:
            if w in self.lastw:
                preds.add(self.lastw[w])
            preds |= self.readers.get(w, set())
        preds.discard(i)
        self.ops.append((e, fn, tuple(reads), tuple(writes), dur, preds, tbl))
        for r in reads:
            self.readers.setdefault(r, set()).add(i)
        for w in writes:
            self.lastw[w] = i
            self.readers[w] = set()

    def flush(self):
        n = len(self.ops)
        fin = [None] * n
        free = {}
        cur_tbl = None
        remaining = list(range(n))
        while remaining:
            best = None
            best_st = None
            for i in remaining[:self.window]:
                e, fn, rd, wr, dur, preds, tbl = self.ops[i]
                ok = True
                st = free.get(e, 0.0)
                for p in preds:
                    if fin[p] is None:
                        ok = False
                        break
                    lat = 0.0 if (self.ops[p][0] == e) else self.LAT
                    st = max(st, fin[p] + lat)
                if not ok:
                    continue
                if tbl is not None and cur_tbl is not None and tbl != cur_tbl:
                    st += 1.3
                if best is None or st < best_st - 1e-9:
                    best, best_st = i, st
            e, fn, rd, wr, dur, preds, tbl = self.ops[best]
            if tbl is not None:
                cur_tbl = tbl
            fin[best] = best_st + dur
            free[e] = fin[best]
            remaining.remove(best)
            self.k.op(e, fn, reads=rd, writes=wr)
        self.ops = []
        self.lastw = {}
        self.readers = {}
        return max(free.values()) if free else 0.0


def apx(base, dims):
    return bass.AP(base.tensor, base.offset, [list(base.ap[0])] + [list(d) for d in dims])


class Carver:
    def __init__(self, work_ap):
        self.w = work_ap
        self.pos = 0

    def f32(self, n):
        a = self.w[:, self.pos:self.pos + n]
        self.pos += n
        assert self.pos <= WORK_WORDS, self.pos
        return a

    def bf16(self, n):
        assert n % 2 == 0
        return self.f32(n // 2).bitcast(BF16)


def build_program(dbg=None):
    nc = bass.Bass("TRN2", target_bir_lowering=False)
    x = nc.dram_tensor("x", [T, D], F32, kind="ExternalInput").ap()
    w_in = nc.dram_tensor("w_in", [D, IN_COLS], F32, kind="ExternalInput").ap()
    w_out = nc.dram_tensor("w_out", [2048, D], F32, kind="ExternalInput").ap()
    w_up = nc.dram_tensor("w_up", [D, 2 * FFN], F32, kind="ExternalInput").ap()
    w_down = nc.dram_tensor("w_down", [FFN, D], F32, kind="ExternalInput").ap()
    cpack_d = nc.dram_tensor("cpack", [128, NCP], F32, kind="ExternalInput").ap()
    bias_d = nc.dram_tensor("biasT", [128, 8 * 384], F32, kind="ExternalInput").ap()
    out = nc.dram_tensor("out", [T, D], F32, kind="ExternalOutput").ap()
    x2d = nc.dram_tensor("x2s", [T, D], F32, kind="Internal").ap()
    dbg_out = None
    if dbg is not None:
        dbg_out = nc.dram_tensor("dbg", [128, 16 * 2048], F32, kind="ExternalOutput").ap()

    with ExitStack() as es:
        hT_t = es.enter_context(nc.sbuf_tensor("hT", [128, 8, T], BF16))
        big_t = es.enter_context(nc.sbuf_tensor("big", [128, 45056], BF16))
        work_t = es.enter_context(nc.sbuf_tensor("work", [128, WORK_WORDS], F32))
        cp_t = es.enter_context(nc.sbuf_tensor("cp", [128, NCP], F32))
        bias_t = es.enter_context(nc.sbuf_tensor("biasbf", [128, 8, 384], BF16))
        cst_t = es.enter_context(nc.sbuf_tensor("cst", [128, 512], BF16))
        sm_t = es.enter_context(nc.sbuf_tensor("sm", [128, 128], F32))
        ps_t = es.enter_context(nc.psum_tensor("ps", [128, 8, 512], F32))
        k = KB(nc, es)
        es.enter_context(nc.Block())

        hT = hT_t[:]
        big = big_t[:]
        work = work_t[:]
        cp = cp_t[:]
        biasb = bias_t[:]
        sm = sm_t[:]
        ident_bf = cst_t[:, 0:128]
        ones_bf = cst_t[:, 128:256]
        mixT = big[:, 0:32768].rearrange("p (c t) -> p c t", c=16)
        wo0 = big[:, 32768:32768 + 8192].rearrange("p (c n) -> p c n", c=16)
        PB = [ps_t[:, i, :] for i in range(8)]
        PBbf = [ps_t[:, i, :].bitcast(BF16) for i in range(8)]

        R_hT = Res("hT")
        R_mix = Res("mixT")
        R_big2 = Res("big2")
        R_cp = Res("cp")
        R_bias = Res("bias")
        R_cst = Res("cst")
        R_sm = Res("sm")
        R_PB = [Res("pb%d" % i, excl=True) for i in range(8)]
        R_x2d = [Res("x2d%d" % i) for i in range(NB)]
        R_out = [Res("out%d" % i) for i in range(4)]
        R_dbg = Res("dbg")
        dma_res = []

        def wres(name):
            r = Res(name)
            dma_res.append(r)
            return r

        k.dma("sp", cp, cpack_d[:, :], writes=[R_cp])
        for hh in range(2):
            k.dma("pool", biasb[:, 4 * hh:4 * hh + 4, :],
                  bias_d[:, 1536 * hh:1536 * (hh + 1)].rearrange("p (h n) -> p h n", h=4), writes=[R_bias])
        k.op("dve", lambda e: e.tensor_copy(out=ident_bf, in_=cp[:, C_ID:C_ID + 128]), reads=[R_cp], writes=[R_cst])
        k.op("dve", lambda e: e.tensor_copy(out=ones_bf, in_=cp[:, C_ONE:C_ONE + 128]), reads=[R_cp], writes=[R_cst])
        U_f = cp[:, C_U:C_U + 128]
        L_f = cp[:, C_L:C_L + 128]
        ones_f = cp[:, C_ONE:C_ONE + 128]
        NEGLAM = sm[:, 0:1]
        SUBG8 = sm[:, 1:2]
        NEGA = sm[:, 8:24]
        junk64 = sm[:, 32:64]
        k.op("dve", lambda e: e.memset(sm[:, 0:8], 0.0), writes=[R_sm])
        lq = sm[:, 64:128]
        R_tmp0 = Res("tmp0")
        for i in range(2):
            a = cp[:, C_LAM + 128 * i:C_LAM + 128 * i + 64]
            b = cp[:, C_LAM + 128 * i + 64:C_LAM + 128 * i + 128]
            k.op("dve", lambda e, a=a, b=b: e.tensor_tensor(out=lq, in0=a, in1=b, op=ALU.mult),
                 reads=[R_cp], writes=[R_tmp0])
            k.op("dve", lambda e, i=i: e.tensor_reduce(out=sm[:, 2 + i:3 + i], in_=lq, axis=mybir.AxisListType.X,
                                                        op=ALU.add), reads=[R_tmp0], writes=[R_sm])
        k.op("act", lambda e: e.activation(out=sm[:, 4:6], in_=sm[:, 2:4], func=AF.Exp), reads=[R_sm], writes=[R_sm])
        k.op("dve", lambda e: e.tensor_tensor(out=sm[:, 0:1], in0=sm[:, 5:6], in1=sm[:, 4:5], op=ALU.subtract),
             reads=[R_sm], writes=[R_sm])
        k.op("dve", lambda e: e.tensor_scalar(out=sm[:, 0:1], in0=sm[:, 0:1], scalar1=-LAM_INIT, scalar2=None,
                                               op0=ALU.add), reads=[R_sm], writes=[R_sm])
        k.op("dve", lambda e: e.tensor_scalar(out=SUBG8, in0=cp[:, C_SUBG:C_SUBG + 1], scalar1=1.0 - LAM_INIT,
                                               scalar2=None, op0=ALU.mult), reads=[R_cp, R_sm], writes=[R_sm])
        k.op("act", lambda e: e.activation(out=NEGA, in_=cp[:, C_ALOG:C_ALOG + 16], func=AF.Exp),
             reads=[R_cp, R_sm], writes=[R_sm])
        k.op("dve", lambda e: e.tensor_scalar(out=NEGA, in0=NEGA, scalar1=-1.0, scalar2=None, op0=ALU.mult),
             reads=[R_sm], writes=[R_sm])

        def rsqrt_small(v_ap, out_ap, res, em=None):
            em = em or k
            kw = dict(dur=0.2, tbl="E") if em is not k else {}
            kw2 = dict(dur=0.2, tbl="E") if em is not k else {}
            em.op("act", lambda e: e.activation(out=out_ap, in_=v_ap, func=AF.Ln), reads=[res], writes=[res], **kw)
            em.op("act", lambda e: e.activation(out=out_ap, in_=out_ap, func=AF.Exp, scale=-0.5), reads=[res],
                  writes=[res], **kw2)

        def load_w(dst, src_rows_ap, res, q="pool"):
            k.dma(q, dst, src_rows_ap.rearrange("(c p) n -> p c n", p=128), writes=[res])

        def norm_stats(src_ap, src_res, cv):
            ss, v, rstd, junk, hb, R_s, R_j, R_hb = cv
            k.op("act", lambda e: e.activation(out=junk, in_=src_ap, func=AF.Square, accum_out=ss),
                 reads=[src_res], writes=[R_j, R_s])
            k.op("dve", lambda e: e.tensor_scalar(out=v, in0=ss, scalar1=1.0 / D, scalar2=EPS, op0=ALU.mult,
                                                   op1=ALU.add), reads=[R_s], writes=[R_s])

        def norm_block_to_hT(src_ap, src_res, blk, gcol0, pbi, cv, stats_done=False):
            ss, v, rstd, junk, hb, R_s, R_j, R_hb = cv
            if not stats_done:
                norm_stats(src_ap, src_res, cv)
            rsqrt_small(v, rstd, R_s)
            k.op("dve", lambda e: e.tensor_scalar(out=hb, in0=src_ap, scalar1=rstd, scalar2=None, op0=ALU.mult),
                 reads=[src_res, R_s], writes=[R_hb])
            pv = PBbf[pbi].rearrange("p (c t) -> p c t", c=8)
            for c in range(8):
                k.op("pe", lambda e, c=c: e.transpose(out=pv[:, c, :], in_=hb[:, c * 128:(c + 1) * 128],
                                                      identity=ident_bf),
                     reads=[R_hb, R_cst], writes=[R_PB[pbi]])
            gb = apx(cp[:, gcol0:gcol0 + 1], [[1, 8], [0, 128]])
            k.op("dve", lambda e: e.tensor_tensor(out=hT[:, :, blk * 128:(blk + 1) * 128], in0=pv, in1=gb,
                                                  op=ALU.mult),
                 reads=[R_PB[pbi], R_cp], writes=[R_hT])

        cv = Carver(work)
        XB = [cv.f32(1024) for _ in range(3)]
        R_XB = [wres("xb%d" % i) for i in range(3)]
        A_sets = []
        for i in range(2):
            s = cv.f32(4)
            A_sets.append((s[:, 0:1], s[:, 1:2], s[:, 2:3], cv.bf16(1024), cv.bf16(1024), Res("a_s%d" % i),
                           Res("a_j%d" % i), Res("a_hb%d" % i)))
        k.dma("sp", XB[0], x[0:128, :], writes=[R_XB[0]])
        norm_stats(XB[0], R_XB[0], A_sets[0])
        for b in range(NB):
            if b + 1 < NB:
                k.dma("sp", XB[(b + 1) % 3], x[(b + 1) * 128:(b + 2) * 128, :], writes=[R_XB[(b + 1) % 3]])
                norm_stats(XB[(b + 1) % 3], R_XB[(b + 1) % 3], A_sets[(b + 1) % 2])
            norm_block_to_hT(XB[b % 3], R_XB[b % 3], b, C_GA, b % 2, A_sets[b % 2], stats_done=True)

        if dbg == "A":
            k.barrier(dma_res)
            cvd = Carver(work)
            t = cvd.f32(8 * 2048 // 2 * 2)
            k.op("dve", lambda e: e.tensor_copy(out=work[:, 0:16384], in_=hT.rearrange("p c t -> p (c t)")),
                 reads=[R_hT], writes=[R_dbg])
            k.dma("sp", dbg_out[:, 0:16384], work[:, 0:16384], reads=[R_dbg])
            k.eng["sp"].wait_ge(k.dsems[R_dbg.dsem], R_dbg.dcnt)
            return nc

        k.barrier(dma_res)
        cv = Carver(work)
        Wq = cv.bf16(2048).rearrange("p (c n) -> p c n", c=8)
        Wk = cv.bf16(2048).rearrange("p (c n) -> p c n", c=8)
        Wv = cv.bf16(2048).rearrange("p (c n) -> p c n", c=8)
        R_Wq, R_Wk, R_Wv = wres("wq"), wres("wk"), wres("wv")
        QT2 = [cv.bf16(2048) for _ in range(2)]
        KT2 = [cv.bf16(4096).rearrange("p (m t) -> p m t", m=2) for _ in range(2)]
        V2 = [cv.bf16(16 * 130).rearrange("p (b e) -> p b e", b=16) for _ in range(2)]
        R_QT2 = [Res("qt0"), Res("qt1")]
        R_KT2 = [Res("kt0"), Res("kt1")]
        R_V2 = [Res("v0"), Res("v1")]
        PT = [cv.bf16(512).rearrange("p (m q) -> p m q", m=2) for _ in range(3)]
        R_PT = [Res("pt%d" % i) for i in range(3)]
        a_all = cv.f32(2048).rearrange("p (j e) -> p j e", j=16)
        R_aall = Res("a_all")
        y_all = cv.bf16(2048).rearrange("p (j e) -> p j e", j=16)
        R_yall = Res("y_all")
        sml = cv.f32(64)
        ss_all = sml[:, 0:16]
        rstd_all = sml[:, 16:32]
        R_ssall = Res("ss_all")
        RR = [sml[:, 32 + 4 * i:36 + 4 * i] for i in range(4)]
        R_RR = [Res("rr%d" % i) for i in range(4)]
        A1 = [cv.f32(128) for _ in range(2)]
        R_A1 = [Res("a1_%d" % i) for i in range(2)]
        jk = cv.f32(128)
        R_jk = Res("jk")

        for i in range(2):
            k.op("dve", lambda e, i=i: e.memset(V2[i][:, :, 128:130], 1.0), writes=[R_V2[i]])
            k.op("dve", lambda e, i=i: e.memset(KT2[i][64:128, 0, :], 0.0), writes=[R_KT2[i]])
            k.op("dve", lambda e, i=i: e.memset(KT2[i][0:64, 1, :], 0.0), writes=[R_KT2[i]])

        evac_i = [0]

        def evac(out_ap, in_ap, reads, writes, scale=None):
            evac_i[0] += 1
            if evac_i[0] % 2 == 0:
                if scale is None:
                    k.op("act", lambda e: e.activation(out=out_ap, in_=in_ap, func=AF.Copy), reads=reads,
                         writes=writes)
                else:
                    k.op("act", lambda e: e.activation(out=out_ap, in_=in_ap, func=AF.Copy, scale=scale),
                         reads=reads, writes=writes)
            else:
                if scale is None:
                    k.op("dve", lambda e: e.tensor_copy(out=out_ap, in_=in_ap), reads=reads, writes=writes)
                else:
                    k.op("dve", lambda e: e.tensor_scalar(out=out_ap, in0=in_ap, scalar1=scale, scalar2=None,
                                                          op0=ALU.mult), reads=reads, writes=writes)

        n_heads = 8
        if dbg == "B1":
            n_heads = 2
        if dbg in ("C", "C1"):
            n_heads = 0

        def load_pair(pair):
            load_w(Wq, w_in[:, Q0 + pair * 256:Q0 + (pair + 1) * 256], R_Wq)
            load_w(Wk, w_in[:, K0 + pair * 256:K0 + (pair + 1) * 256], R_Wk)
            load_w(Wv, w_in[:, V0 + pair * 256:V0 + (pair + 1) * 256], R_Wv)

        pj_i = [0]

        def evac_dve(out_ap, in_ap, reads, writes, scale=None):
            if scale is None:
                k.op("dve", lambda e: e.tensor_copy(out=out_ap, in_=in_ap), reads=reads, writes=writes)
            else:
                k.op("dve", lambda e: e.tensor_scalar(out=out_ap, in0=in_ap, scalar1=scale, scalar2=None,
                                                      op0=ALU.mult), reads=reads, writes=writes)

        def proj_units(h):
            hh = h % 2
            buf = h % 2
            mops = []
            for kind in ("q", "k"):
                for tt in range(4):
                    pbi = 6 + pj_i[0] % 2
                    pj_i[0] += 1
                    Wt, R_W = (Wq, R_Wq) if kind == "q" else (Wk, R_Wk)
                    for c in range(8):
                        mops.append((0.22, lambda c=c, pbi=pbi, Wt=Wt, R_W=R_W, tt=tt: k.op(
                            "pe", lambda e: e.matmul(PB[pbi], lhsT=Wt[:, c, hh * 128:(hh + 1) * 128],
                                                     rhs=hT[:, c, tt * 512:(tt + 1) * 512], start=(c == 0),
                                                     stop=(c == 7)),
                            reads=[R_W, R_hT], writes=[R_PB[pbi]])))
                    if kind == "q":
                        mops.append((0.0, lambda pbi=pbi, tt=tt: evac_dve(
                            QT2[buf][:, tt * 512:(tt + 1) * 512], PB[pbi], [R_PB[pbi]], [R_QT2[buf]], scale=0.125)))
                    else:
                        for m in range(2):
                            mops.append((0.0, lambda pbi=pbi, tt=tt, m=m: evac_dve(
                                KT2[buf][m * 64:(m + 1) * 64, m, tt * 512:(tt + 1) * 512],
                                PB[pbi][m * 64:(m + 1) * 64, :], [R_PB[pbi]], [R_KT2[buf]])))
            for b in range(NB):
                pbi = 6 + pj_i[0] % 2
                pj_i[0] += 1
                for c in range(8):
                    mops.append((0.06, lambda c=c, pbi=pbi, b=b: k.op(
                        "pe", lambda e: e.matmul(PB[pbi][:, 0:128], lhsT=hT[:, c, b * 128:(b + 1) * 128],
                                                 rhs=Wv[:, c, hh * 128:(hh + 1) * 128], start=(c == 0),
                                                 stop=(c == 7)),
                        reads=[R_Wv, R_hT], writes=[R_PB[pbi]])))
                mops.append((0.0, lambda pbi=pbi, b=b: evac_dve(V2[buf][:, b, 0:128], PB[pbi][:, 0:128],
                                                                [R_PB[pbi]], [R_V2[buf]])))
            if hh == 1 and h + 1 < n_heads:
                mops.append((0.0, lambda: load_pair((h + 1) // 2)))
            return mops

        tiles = [(h, J, kb) for h in range(n_heads) for J in range(8) for kb in range(2 * J + 2)]

        def tile_ctx(n):
            h, J, kb = tiles[n]
            q0 = 128 if kb == 2 * J + 1 else 0
            sti = n % 2
            st = PB[sti].rearrange("p (m q) -> p m q", m=2)
            return h, J, kb, q0, sti, st, PT[n % 3], R_PT[n % 3]

        def stage1(n):
            h, J, kb, q0, sti, st, pt, R_pt = tile_ctx(n)
            buf = h % 2
            QT, KT = QT2[buf], KT2[buf]
            far_b = cp[:, C_FAR + h:C_FAR + h + 1]
            near = kb >= 2 * J - 1
            if near:
                if kb == 2 * J - 1:
                    bsl = biasb[:, h, 128:384]
                elif kb == 2 * J:
                    bsl = biasb[:, h, 0:256]
                else:
                    bsl = biasb[:, h, 0:128]
            for m in range(2):
                k.op("pe", lambda e, m=m: e.matmul(
                    st[:, m, q0:256], lhsT=KT[:, m, kb * 128:(kb + 1) * 128],
                    rhs=QT[:, J * 256 + q0:(J + 1) * 256], start=True, stop=not near,
                    skip_group_check=True), reads=[R_KT2[buf], R_QT2[buf]], writes=[R_PB[sti]])
                if near:
                    k.op("pe", lambda e, m=m: e.matmul(
                        st[:, m, q0:256], lhsT=ident_bf, rhs=bsl, start=False, stop=True,
                        skip_group_check=True), reads=[R_cst, R_bias], writes=[R_PB[sti]])
            if near:
                k.op("act", lambda e: e.activation(out=pt[:, :, q0:256], in_=st[:, :, q0:256], func=AF.Exp),
                     reads=[R_PB[sti]], writes=[R_pt])
            else:
                k.op("act", lambda e: e.activation(out=pt[:, :, :], in_=st[:, :, :], func=AF.Exp, bias=far_b),
                     reads=[R_PB[sti], R_cp], writes=[R_pt])

        def stage2(n):
            h, J, kb, q0, sti, st, pt, R_pt = tile_ctx(n)
            buf = h % 2
            Vt = V2[buf]
            accb = [2 + 2 * (J % 2), 3 + 2 * (J % 2)]
            accv = [PB[i][:, 0:260].rearrange("p (m e) -> p m e", m=2) for i in accb]
            for jj in range(2):
                if jj * 128 < q0:
                    continue
                last_kb = 2 * J + jj
                for m in range(2):
                    k.op("pe", lambda e, jj=jj, m=m: e.matmul(
                        accv[jj][:, m, 0:129], lhsT=pt[:, m, jj * 128:(jj + 1) * 128],
                        rhs=Vt[:, kb, 0:129], start=(kb == 0 and m == 0), stop=(kb == last_kb),
                        skip_group_check=True), reads=[R_pt, R_V2[buf]], writes=[R_PB[accb[jj]]])
                if kb == last_kb:
                    jb = 2 * J + jj
                    acc = accv[jj]
                    R_acc = R_PB[accb[jj]]
                    rr, R_rr = RR[jb % 4], R_RR[jb % 4]
                    a1, R_a1 = A1[jb % 2], R_A1[jb % 2]
                    k.op("dve", lambda e, acc=acc, rr=rr: e.reciprocal(out=rr[:, 0:2], in_=acc[:, :, 128]),
                         reads=[R_acc], writes=[R_rr])
                    k.op("dve", lambda e, rr=rr: e.tensor_tensor(out=rr[:, 2:3], in0=rr[:, 1:2], in1=NEGLAM,
                                                                 op=ALU.mult),
                         reads=[R_rr, R_sm], writes=[R_rr])
                    k.op("dve", lambda e, acc=acc, rr=rr, a1=a1: e.tensor_scalar(
                        out=a1, in0=acc[:, 0, 0:128], scalar1=rr[:, 0:1], scalar2=None, op0=ALU.mult),
                         reads=[R_acc, R_rr], writes=[R_a1])
                    k.op("dve", lambda e, acc=acc, rr=rr, a1=a1, jb=jb: e.scalar_tensor_tensor(
                        out=a_all[:, jb, :], in0=acc[:, 1, 0:128], scalar=rr[:, 2:3], in1=a1,
                        op0=ALU.mult, op1=ALU.add), reads=[R_acc, R_rr, R_a1], writes=[R_aall])
                    k.op("dve", lambda e, jb=jb: e.tensor_tensor(out=jk, in0=a_all[:, jb, :],
                                                                 in1=a_all[:, jb, :], op=ALU.mult),
                         reads=[R_aall], writes=[R_jk])
                    k.op("dve", lambda e, jb=jb: e.tensor_reduce(out=ss_all[:, jb:jb + 1], in_=jk,
                                                                 axis=mybir.AxisListType.X, op=ALU.add),
                         reads=[R_jk], writes=[R_ssall])
            if J == 7 and kb == 15:
                k.op("dve", lambda e: e.tensor_scalar(out=rstd_all, in0=ss_all, scalar1=1.0 / 128,
                                                       scalar2=SUB_EPS, op0=ALU.mult, op1=ALU.add),
                     reads=[R_ssall], writes=[R_ssall])
                rsqrt_small(rstd_all, rstd_all, R_ssall)
                rb = apx(rstd_all[:, 0:1], [[1, 16], [0, 128]])
                k.op("dve", lambda e: e.tensor_tensor(out=y_all, in0=a_all, in1=rb, op=ALU.mult),
                     reads=[R_aall, R_ssall], writes=[R_yall])
                def epi_pe(h=h):
                    for half in range(2):
                        pbi = 6 + half
                        pv = PBbf[pbi].rearrange("p (j q) -> p j q", j=8)
                        for j8 in range(8):
                            k.op("pe", lambda e, pv=pv, j8=j8, half=half: e.transpose(
                                out=pv[:, j8, :], in_=y_all[:, half * 8 + j8, :], identity=ident_bf),
                                reads=[R_yall, R_cst], writes=[R_PB[pbi]])
                        k.op("dve", lambda e, half=half, pbi=pbi: e.tensor_scalar(
                            out=mixT[:, h, half * 1024:(half + 1) * 1024], in0=PBbf[pbi], scalar1=SUBG8,
                            scalar2=None, op0=ALU.mult), reads=[R_PB[pbi], R_sm], writes=[R_mix])
                if h + 1 < n_heads:
                    epi_pending.append(epi_pe)
                else:
                    epi_pe()

        deferred = []
        epi_pending = []
        if n_heads > 0:
            load_pair(0)
            for _, u in proj_units(0):
                u()
            pend = []
            for n in range(len(tiles)):
                h, J, kb = tiles[n]
                if J == 0 and kb == 0:
                    pend = proj_units(h + 1) if h + 1 < n_heads else []
                    if epi_pending:
                        pend = [(0.9, epi_pending.pop())] + pend
                    total_cost = sum(c for c, _ in pend)
                    rate = total_cost / 62.0
                    budget = 0.0
                    done_u = 0
                if n == 0:
                    stage1(0)
                if n + 1 < len(tiles):
                    stage1(n + 1)
                budget += rate
                while done_u < len(pend) and (pend[done_u][0] <= budget or (J == 7 and kb >= 13)):
                    budget -= pend[done_u][0]
                    pend[done_u][1]()
                    done_u += 1
                stage2(n)

        if dbg in ("B", "B1"):
            k.barrier(dma_res)
            k.op("dve", lambda e: e.tensor_copy(out=work[:, 0:16384], in_=big[:, 0:16384]),
                 reads=[R_mix], writes=[R_dbg])
            k.dma("sp", dbg_out[:, 0:16384], work[:, 0:16384], reads=[R_dbg])
            k.eng["sp"].wait_ge(k.dsems[R_dbg.dsem], R_dbg.dcnt)
            return nc

        k.barrier(dma_res)
        cv = Carver(work)
        xsT = cv.bf16(8192).rearrange("p (c t) -> p c t", c=4)
        bcT = cv.bf16(4096).rearrange("p (c t) -> p c t", c=2)
        R_xsT, R_bcT = Res("xsT"), Res("bcT")
        Wxz = cv.bf16(4096).rearrange("p (c n) -> p c n", c=8)
        Wdt = cv.bf16(128).rearrange("p (c n) -> p c n", c=8)
        R_Wxz, R_Wbc, R_Wdt = wres("wxz"), wres("wbc"), wres("wdt")
        dt_all = cv.f32(256).rearrange("p (b h) -> p b h", b=16)
        edc_all = cv.f32(768).rearrange("p (b h) -> p b h", b=16)
        ahi_all = cv.bf16(256).rearrange("p (b h) -> p b h", b=16)
        alo_all = cv.bf16(256).rearrange("p (b h) -> p b h", b=16)
        R_dt = Res("dt_all")
        R_edc = Res("edc_all")
        R_ahl = Res("ahl")
        temp0 = cv.pos
        acc_c2 = [cv.f32(2048), cv.f32(2048)]
        Wbc = cv.bf16(2048).rearrange("p (c n) -> p c n", c=8)
        R_accc2 = [Res("acc_c0"), Res("acc_c1")]
        cv.pos = temp0
        rhsH = cv.bf16(1024)
        t1_alt = cv.f32(512)
        dec = cv.bf16(1024)
        MT2 = [cv.bf16(1024) for _ in range(2)]
        xdt2 = [cv.bf16(512) for _ in range(2)]
        xdd2 = [cv.bf16(512) for _ in range(2)]
        bmsb2 = [cv.bf16(128) for _ in range(2)]
        t1 = cv.f32(512)
        t1b = cv.bf16(512)
        R_t1b = Res("t1b")
        diagD = [cv.bf16(128) for _ in range(4)]
        sz = cv.f32(512)
        sz_alt = cv.f32(512)
        o_tm = cv.bf16(512)
        S_f = cv.f32(512)
        S_b = cv.bf16(512)
        cbm = cv.bf16(128)
        smc = cv.f32(48)
        R_rhs, R_dec = Res("rhsHL"), Res("dec")
        R_MT2 = [Res("MT%d" % i) for i in range(2)]
        R_xdt2 = [Res("xdt%d" % i) for i in range(2)]
        R_xdd2 = [Res("xdd%d" % i) for i in range(2)]
        R_bmsb2 = [Res("bmsb%d" % i) for i in range(2)]
        R_t1, R_dD, R_sz, R_otm = Res("t1"), Res("diagD"), Res("sz"), Res("otm")
        t1_2 = [t1, t1_alt]
        sz_2 = [sz, sz_alt]
        R_sz_2 = [R_sz, Res("sz2")]
        R_t1_2 = [R_t1, Res("t1b2")]
        R_S, R_Sb, R_cbm, R_smc = Res("S"), Res("Sb"), Res("cbm"), Res("smc")
        U_bf = cst_t[:, 256:384]
        L_bf = cst_t[:, 384:512]
        k.op("dve", lambda e: e.tensor_copy(out=U_bf, in_=U_f), reads=[R_cp], writes=[R_cst])
        k.op("dve", lambda e: e.tensor_copy(out=L_bf, in_=L_f), reads=[R_cp], writes=[R_cst])

        k.dma("pool", wo0, w_out[:, 0:512].rearrange("(c p) n -> p c n", p=128), writes=[R_big2])

        load_w(Wdt, w_in[:, DT0:DT0 + 16], R_Wdt)
        sc0 = Sched(k)
        for b in range(NB):
            pbi = b % 2
            dtb = smc[:, 16 * pbi:16 * pbi + 16]
            al = sm[:, 96 + 16 * pbi:112 + 16 * pbi]
            R_al = R_smc
            for c in range(8):
                sc0.op("pe", lambda e, c=c, b=b, pbi=pbi: e.matmul(PB[pbi][:, 0:16],
                                                                  lhsT=hT[:, c, b * 128:(b + 1) * 128],
                                                                  rhs=Wdt[:, c, :], start=(c == 0), stop=(c == 7)),
                     reads=[R_Wdt, R_hT], writes=[R_PB[pbi]])
            sc0.op("dve", lambda e, pbi=pbi, dtb=dtb: e.tensor_tensor(out=dtb, in0=PB[pbi][:, 0:16],
                                                                     in1=cp[:, C_DTB:C_DTB + 16], op=ALU.add),
                 reads=[R_PB[pbi], R_cp], writes=[R_smc])
            sc0.op("act", lambda e, dtb=dtb: e.activation(out=dtb, in_=dtb, func=AF.Exp), reads=[R_smc],
                 writes=[R_smc])
            sc0.op("act", lambda e, b=b, dtb=dtb: e.activation(out=dt_all[:, b, :], in_=dtb, func=AF.Ln, bias=1.0),
                 reads=[R_smc], writes=[R_dt])
            sc0.op("dve", lambda e, b=b, al=al: e.tensor_tensor(out=al, in0=dt_all[:, b, :], in1=NEGA, op=ALU.mult),
                 reads=[R_dt, R_sm], writes=[R_al])
            for i, M_ in enumerate((U_f, L_f, ones_f)):
                sc0.op("pe", lambda e, i=i, M_=M_, al=al, pbi=pbi: e.matmul(PB[2 + pbi][:, 16 * i:16 * i + 16],
                                                                          lhsT=M_, rhs=al, start=True, stop=True),
                     reads=[R_al, R_cp], writes=[R_PB[2 + pbi]])
            sc0.op("act", lambda e, b=b, pbi=pbi: e.activation(out=edc_all[:, b, :], in_=PB[2 + pbi][:, 0:48],
                                                             func=AF.Exp), reads=[R_PB[2 + pbi]], writes=[R_edc])
            sc0.op("dve", lambda e, b=b, al=al: e.tensor_copy(out=ahi_all[:, b, :], in_=al), reads=[R_al],
                 writes=[R_ahl])
            sc0.op("dve", lambda e, b=b, al=al: e.tensor_tensor(out=alo_all[:, b, :], in0=al, in1=ahi_all[:, b, :],
                                                              op=ALU.subtract), reads=[R_al, R_ahl],
                 writes=[R_ahl])

        sc0.flush()
        n_groups = 2 if dbg != "C1" else 1
        for g in range(n_groups):
            load_w(Wxz, w_in[:, XBC0 + g * 512:XBC0 + (g + 1) * 512], R_Wxz)
            load_w(Wbc[:, :, 0:128], w_in[:, XBC0 + 1024 + g * 128:XBC0 + 1024 + (g + 1) * 128], R_Wbc)
            load_w(Wbc[:, :, 128:256], w_in[:, XBC0 + 1280 + g * 128:XBC0 + 1280 + (g + 1) * 128], R_Wbc)
            sc1 = Sched(k)
            for ci in range(6):
                if ci < 4:
                    Wt, R_W, cs, chunk_id = Wxz, R_Wxz, ci * 128, g * 4 + ci
                    dst, R_dst = xsT[:, ci, :], R_xsT
                else:
                    Wt, R_W, cs, chunk_id = Wbc, R_Wbc, (ci - 4) * 128, 8 + 2 * (ci - 4) + g
                    dst, R_dst = bcT[:, ci - 4, :], R_bcT
                bset = 4 * (ci % 2)
                acc_c, R_accc = acc_c2[ci % 2], R_accc2[ci % 2]
                u = ps_t[:, bset:bset + 4, :].rearrange("p a b -> p (a b)")
                R_u = [R_PB[bset + i] for i in range(4)]
                for tt in range(4):
                    for c in range(8):
                        sc1.op("pe", lambda e, c=c, tt=tt, Wt=Wt, cs=cs, bset=bset: e.matmul(
                            PB[bset + tt], lhsT=Wt[:, c, cs:cs + 128], rhs=hT[:, c, tt * 512:(tt + 1) * 512],
                            start=(c == 0), stop=(c == 7)), reads=[R_W, R_hT], writes=[R_PB[bset + tt]], dur=0.22)
                cw = cp[:, C_CWS + 4 * chunk_id:C_CWS + 4 * chunk_id + 4]
                cb = cp[:, C_CBS + chunk_id:C_CBS + chunk_id + 1]
                sc1.op("act", lambda e, u=u, cw=cw, cb=cb, acc_c=acc_c: e.activation(out=acc_c, in_=u, func=AF.Identity,
                                                                                   scale=cw[:, 3:4], bias=cb),
                     reads=R_u + [R_cp], writes=[R_accc], dur=2.0)
                for sh in (1, 2, 3):
                    sc1.op("dve", lambda e, u=u, cw=cw, sh=sh, acc_c=acc_c: e.scalar_tensor_tensor(
                        out=acc_c[:, sh:T], in0=u[:, 0:T - sh], scalar=cw[:, 3 - sh:4 - sh], in1=acc_c[:, sh:T],
                        op0=ALU.mult, op1=ALU.add), reads=R_u + [R_cp, R_accc], writes=[R_accc], dur=2.35)
                sc1.op("act", lambda e, dst=dst, acc_c=acc_c: e.activation(out=dst, in_=acc_c, func=AF.Silu),
                     reads=[R_accc], writes=[R_dst], dur=1.9)
            sc1.flush()
            load_w(Wxz, w_in[:, Z0 + g * 512:Z0 + (g + 1) * 512], R_Wxz)
            k.barrier([R_Wbc])
            for i in range(4):
                k.op("dve", lambda e, i=i: e.tensor_scalar(out=diagD[i], in0=ident_bf,
                                                           scalar1=cp[:, C_DCOL + g * 4 + i:C_DCOL + g * 4 + i + 1],
                                                           scalar2=None, op0=ALU.mult),
                     reads=[R_cst, R_cp], writes=[R_dD])
            k.op("dve", lambda e: e.memset(S_f, 0.0), writes=[R_S])
            k.op("dve", lambda e: e.memset(S_b, 0.0), writes=[R_Sb])

            sch = Sched(k)

            def front(b):
                blk = slice(b * 128, (b + 1) * 128)
                p2 = b % 2
                MT, xdt, xdd, bm_sb = MT2[p2], xdt2[p2], xdd2[p2], bmsb2[p2]
                dt_b = dt_all[:, b, g * 8:(g + 1) * 8]
                dte = edc_all[:, b, 16 + g * 8:24 + g * 8]
                for dst, src in ((rhsH, ahi_all),):
                    sch.op("dve", lambda e, dst=dst, src=src: e.tensor_tensor(
                        out=dst.rearrange("p (r l) -> p r l", r=8), in0=apx(U_bf[:, 0:1], [[0, 8], [1, 128]]),
                        in1=apx(src[:, b, g * 8:g * 8 + 1], [[1, 8], [0, 128]]), op=ALU.mult),
                         reads=[R_cst, R_ahl], writes=[R_rhs], dur=1.23)
                for hf in range(2):
                    sch.op("pe", lambda e, hf=hf: e.matmul(PB[3 + hf], lhsT=L_bf, rhs=rhsH[:, hf * 512:(hf + 1) * 512],
                                                         start=True, stop=True),
                         reads=[R_rhs, R_cst], writes=[R_PB[3 + hf]], dur=0.3)
                seg = ps_t[:, 3:5, :].rearrange("p a b -> p (a b)")
                sch.op("act", lambda e: e.activation(out=dec, in_=seg, func=AF.Exp),
                     reads=[R_PB[3], R_PB[4]], writes=[R_dec], dur=1.05, tbl="E")
                xs_tm = PBbf[0][:, 0:512]
                for i in range(4):
                    sch.op("pe", lambda e, i=i: e.transpose(out=xs_tm[:, i * 128:(i + 1) * 128], in_=xsT[:, i, blk],
                                                          identity=ident_bf),
                         reads=[R_xsT, R_cst], writes=[R_PB[0]])
                bm_tm = PBbf[2][:, 512:640]
                sch.op("pe", lambda e: e.matmul(PB[2][:, 0:128], lhsT=bcT[:, 0, blk], rhs=bcT[:, 1, blk],
                                              start=True, stop=True), reads=[R_bcT], writes=[R_PB[2]])
                sch.op("pe", lambda e: e.transpose(out=bm_tm, in_=bcT[:, 0, blk], identity=ident_bf),
                     reads=[R_bcT, R_cst], writes=[R_PB[2]])
                sch.op("dve", lambda e: e.tensor_tensor(out=cbm, in0=PB[2][:, 0:128], in1=U_f, op=ALU.mult),
                     reads=[R_PB[2], R_cp], writes=[R_cbm])
                sch.op("act", lambda e: e.activation(out=bm_sb, in_=bm_tm, func=AF.Copy), reads=[R_PB[2]],
                       writes=[R_bmsb2[p2]], dur=0.3)
                sch.op("dve", lambda e: e.tensor_tensor(out=MT.rearrange("p (r l) -> p r l", r=8),
                                                      in0=dec.rearrange("p (r l) -> p r l", r=8),
                                                      in1=apx(cbm[:, 0:1], [[0, 8], [1, 128]]), op=ALU.mult),
                     reads=[R_dec, R_cbm], writes=[R_MT2[p2]], dur=1.14)
                sch.op("dve", lambda e: e.tensor_tensor(
                    out=xdt.rearrange("p (r q) -> p r q", r=8), in0=xs_tm.rearrange("p (r q) -> p r q", r=8),
                    in1=apx(dt_b[:, 0:1], [[1, 8], [0, 64]]), op=ALU.mult),
                     reads=[R_PB[0], R_dt], writes=[R_xdt2[p2]])
                sch.op("dve", lambda e: e.tensor_tensor(
                    out=xdd.rearrange("p (r q) -> p r q", r=8), in0=xdt.rearrange("p (r q) -> p r q", r=8),
                    in1=apx(dte[:, 0:1], [[1, 8], [0, 64]]), op=ALU.mult),
                     reads=[R_xdt2[p2], R_edc], writes=[R_xdd2[p2]], dur=0.5)
                szf, R_szf = sz_2[p2], R_sz_2[p2]
                for c in range(8):
                    sch.op("pe", lambda e, c=c: e.matmul(PB[7], lhsT=hT[:, c, blk], rhs=Wxz[:, c, :],
                                                       start=(c == 0), stop=(c == 7)),
                         reads=[R_hT, R_Wxz], writes=[R_PB[7]], dur=0.22)
                sch.op("act", lambda e: e.activation(out=szf, in_=PB[7], func=AF.Silu), reads=[R_PB[7]],
                       writes=[R_szf], dur=0.6, tbl="S")

            def back(b):
                blk = slice(b * 128, (b + 1) * 128)
                p2 = b % 2
                t1, R_t1 = t1_2[p2], R_t1_2[p2]
                sz, R_sz = sz_2[p2], R_sz_2[p2]
                MT, xdt, xdd, bm_sb = MT2[p2], xdt2[p2], xdd2[p2], bmsb2[p2]
                xs_tm = PBbf[0][:, 0:512]
                E_ = edc_all[:, b, g * 8:(g + 1) * 8]
                cd = edc_all[:, b, 32 + g * 8:40 + g * 8]
                for r in range(8):
                    sch.op("pe", lambda e, r=r: e.matmul(PB[5][:, r * 64:(r + 1) * 64],
                                                       lhsT=MT[:, r * 128:(r + 1) * 128],
                                                       rhs=xdt[:, r * 64:(r + 1) * 64], start=(r == 0), stop=False,
                                                       skip_group_check=True),
                         reads=[R_MT2[p2], R_xdt2[p2]], writes=[R_PB[5]], dur=0.06)
                for i in range(4):
                    sch.op("pe", lambda e, i=i: e.matmul(PB[5][:, i * 128:(i + 1) * 128], lhsT=xsT[:, i, blk],
                                                        rhs=diagD[i], start=False, stop=False,
                                                        skip_group_check=True),
                           reads=[R_xsT, R_dD], writes=[R_PB[5]], dur=0.11)
                sch.op("pe", lambda e: e.matmul(PB[6], lhsT=bcT[:, 1, blk], rhs=S_b, start=True, stop=True),
                     reads=[R_bcT, R_Sb], writes=[R_PB[6]])
                sch.op("dve", lambda e: e.tensor_tensor(
                    out=t1b.rearrange("p (r q) -> p r q", r=8), in0=PB[6].rearrange("p (r q) -> p r q", r=8),
                    in1=apx(E_[:, 0:1], [[1, 8], [0, 64]]), op=ALU.mult),
                     reads=[R_PB[6], R_edc], writes=[R_t1b])
                sch.op("pe", lambda e: e.matmul(PB[5], lhsT=ident_bf, rhs=t1b, start=False, stop=True,
                                                skip_group_check=True),
                       reads=[R_cst, R_t1b], writes=[R_PB[5]], dur=0.3)
                sch.op("dve", lambda e: e.tensor_tensor(out=t1, in0=PB[5], in1=sz, op=ALU.mult),
                     reads=[R_PB[5], R_sz], writes=[R_t1])
                ssq = smc[:, 32:33]
                vv = smc[:, 33:34]
                sch.op("act", lambda e: e.activation(out=sz, in_=t1, func=AF.Square, accum_out=ssq),
                     reads=[R_t1], writes=[R_sz, R_smc])
                sch.op("dve", lambda e: e.tensor_scalar(out=vv, in0=ssq, scalar1=1.0 / 512, scalar2=SSM_EPS,
                                                         op0=ALU.mult, op1=ALU.add), reads=[R_smc], writes=[R_smc],
                       dur=0.15)
                rsqrt_small(vv, vv, R_smc, em=sch)
                sch.op("act", lambda e: e.activation(out=o_tm, in_=t1, func=AF.Copy, scale=vv),
                       reads=[R_t1, R_smc], writes=[R_otm], dur=0.65)
                ot_ps = PBbf[1][:, 0:512].rearrange("p (c t) -> p c t", c=4)
                for i in range(4):
                    sch.op("pe", lambda e, i=i: e.transpose(out=ot_ps[:, i, :], in_=o_tm[:, i * 128:(i + 1) * 128],
                                                          identity=ident_bf),
                         reads=[R_otm, R_cst], writes=[R_PB[1]])
                sch.op("act", lambda e: e.activation(out=mixT[:, 8 + g * 4:12 + g * 4, b * 128:(b + 1) * 128],
                                                   in_=ot_ps, func=AF.Copy),
                     reads=[R_PB[1]], writes=[R_mix])
                sch.op("pe", lambda e: e.matmul(PB[6], lhsT=bm_sb, rhs=xdd, start=True, stop=True),
                     reads=[R_bmsb2[p2], R_xdd2[p2]], writes=[R_PB[6]])
                sch.op("dve", lambda e: e.tensor_tensor(
                    out=S_f.rearrange("p (r q) -> p r q", r=8), in0=S_f.rearrange("p (r q) -> p r q", r=8),
                    in1=apx(cd[:, 0:1], [[1, 8], [0, 64]]), op=ALU.mult), reads=[R_S, R_edc], writes=[R_S])
                sch.op("dve", lambda e: e.tensor_tensor(out=S_f, in0=PB[6], in1=S_f, op=ALU.add),
                     reads=[R_PB[6], R_S], writes=[R_S])
                sch.op("act", lambda e: e.activation(out=S_b, in_=S_f, func=AF.Copy), reads=[R_S], writes=[R_Sb],
                       dur=0.65)

            for b in range(NB):
                front(b)
                back(b)
            est = sch.flush()
            print('[sched] ssd group', g, 'estimated us', round(est, 1))
            if g == 0:
                k.barrier([])

        if dbg in ("C", "C1"):
            k.barrier(dma_res)
            k.op("dve", lambda e: e.tensor_copy(out=work[:, 0:16384], in_=big[:, 16384:32768]),
                 reads=[R_mix], writes=[R_dbg])
            k.dma("sp", dbg_out[:, 0:16384], work[:, 0:16384], reads=[R_dbg])
            k.eng["sp"].wait_ge(k.dsems[R_dbg.dsem], R_dbg.dcnt)
            return nc

        k.barrier(dma_res)
        cv = Carver(work)
        wo1 = cv.bf16(8192).rearrange("p (c n) -> p c n", c=16)
        R_wo1 = wres("wo1")
        k.dma("pool", wo1, w_out[:, 512:1024].rearrange("(c p) n -> p c n", p=128), writes=[R_wo1])
        for wo, R_wo in ((wo0, R_big2), (wo1, R_wo1)):
            for c in range(8, 16):
                k.op("dve", lambda e, wo=wo, c=c: e.tensor_scalar(
                    out=wo[:, c, :], in0=wo[:, c, :], scalar1=cp[:, C_NGCOL + c - 8:C_NGCOL + c - 7], scalar2=None,
                    op0=ALU.mult), reads=[R_wo, R_cp], writes=[R_wo])
        XB = [cv.f32(1024) for _ in range(2)]
        R_XB = [wres("dxb%d" % i) for i in range(2)]
        X2 = [cv.f32(1024) for _ in range(2)]
        R_X2 = [wres("dx2%d" % i) for i in range(2)]
        D_sets = []
        for i in range(2):
            s = cv.f32(4)
            D_sets.append((s[:, 0:1], s[:, 1:2], s[:, 2:3], cv.bf16(1024), cv.bf16(1024), Res("d_s%d" % i),
                           Res("d_j%d" % i), Res("d_hb%d" % i)))
        def d_mm(b):
            xb, R_xb = XB[b % 2], R_XB[b % 2]
            x2, R_x2 = X2[b % 2], R_X2[b % 2]
            k.dma("sp", xb, x[b * 128:(b + 1) * 128, :], writes=[R_xb])
            for half, (wo, R_wo) in enumerate(((wo0, R_big2), (wo1, R_wo1))):
                pbi = 2 * (b % 2) + half
                for c in range(16):
                    k.op("pe", lambda e, c=c, pbi=pbi, wo=wo, b=b: e.matmul(
                        PB[pbi], lhsT=mixT[:, c, b * 128:(b + 1) * 128], rhs=wo[:, c, :], start=(c == 0),
                        stop=(c == 15)), reads=[R_mix, R_wo], writes=[R_PB[pbi]])
                k.op("dve", lambda e, half=half, pbi=pbi, xb=xb, x2=x2: e.tensor_tensor(
                    out=x2[:, half * 512:(half + 1) * 512], in0=PB[pbi], in1=xb[:, half * 512:(half + 1) * 512],
                    op=ALU.add), reads=[R_PB[pbi], R_xb], writes=[R_x2])

        def d_norm(b):
            x2, R_x2 = X2[b % 2], R_X2[b % 2]
            k.dma("pool", x2d[b * 128:(b + 1) * 128, :], x2, reads=[R_x2], writes=[R_x2d[b]])
            norm_block_to_hT(x2, R_x2, b, C_GF, 4 + b % 2, D_sets[b % 2], stats_done=True)

        d_mm(0)
        norm_stats(X2[0], R_X2[0], D_sets[0])
        for b in range(NB):
            if b + 1 < NB:
                d_mm(b + 1)
                norm_stats(X2[(b + 1) % 2], R_X2[(b + 1) % 2], D_sets[(b + 1) % 2])
            d_norm(b)

        if dbg == "D":
            k.barrier(dma_res + R_x2d)
            k.op("dve", lambda e: e.tensor_copy(out=work[:, 0:16384], in_=hT.rearrange("p c t -> p (c t)")),
                 reads=[R_hT], writes=[R_dbg])
            k.dma("sp", dbg_out[:, 0:16384], work[:, 0:16384], reads=[R_dbg])
            k.eng["sp"].wait_ge(k.dsems[R_dbg.dsem], R_dbg.dcnt)
            return nc

        k.barrier(dma_res + R_x2d)
        actT = big[:, 0:22528].rearrange("p (c t) -> p c t", c=22)
        wd = big[:, 22528:45056].rearrange("p (c n) -> p c n", c=22)
        R_actT = Res("actT")
        R_wd = Res("wd")
        cv = Carver(work)
        NWB = 3
        WG = [cv.bf16(2048).rearrange("p (c n) -> p c n", c=8) for _ in range(NWB)]
        WV = [cv.bf16(2048).rearrange("p (c n) -> p c n", c=8) for _ in range(NWB)]
        R_WG = [wres("wg%d" % i) for i in range(NWB)]
        R_WV = [wres("wv%d" % i) for i in range(NWB)]

        def load_up(seq):
            j = seq % 11
            load_w(WG[seq % NWB], w_up[:, 256 * j:256 * (j + 1)], R_WG[seq % NWB])
            load_w(WV[seq % NWB], w_up[:, FFN + 256 * j:FFN + 256 * (j + 1)], R_WV[seq % NWB])

        load_up(0)
        load_up(1)
        ACC = [cv.f32(1028) for _ in range(2)]
        R_ACC = [Res("acc%d" % i) for i in range(2)]
        SG = cv.f32(1024)
        R_SG = Res("sg")
        FX = [cv.f32(1024) for _ in range(2)]
        R_FX = [wres("fx%d" % i) for i in range(2)]
        FY = [cv.f32(1024) for _ in range(2)]
        R_FY = [wres("fy%d" % i) for i in range(2)]
        fj = cv.bf16(1024)
        R_fj = Res("fj")
        fs = cv.f32(8)
        R_fs = Res("fs")

        ch_i = 0
        oblk = 0
        for hf in range(2):
            halo = 2 if hf == 1 else 0
            t0 = hf * 1024 - halo
            ncol = 1024 + halo
            tiles = []
            c0 = 0
            while c0 < ncol:
                n = min(512, ncol - c0)
                tiles.append((c0, n))
                c0 += n
            for j in range(11):
                seq = hf * 11 + j
                wg, R_wg = WG[seq % NWB], R_WG[seq % NWB]
                wv, R_wv = WV[seq % NWB], R_WV[seq % NWB]
                if seq + 2 < 22:
                    load_up(seq + 2)
                if seq == 1:
                    for i2 in range(2):
                        k.dma("pool", wd[:, 11 * i2:11 * (i2 + 1), :],
                              w_down[1408 * i2:1408 * (i2 + 1), :].rearrange("(c p) n -> p c n", p=128),
                              writes=[R_wd])
                for sub in range(2):
                    i = 2 * j + sub
                    ctxs = []
                    for kind, (Wt, R_W) in enumerate(((wg, R_wg), (wv, R_wv))):
                        nbk = len(tiles)
                        bset = nbk * (ch_i % (6 // nbk))
                        ch_i += 1
                        u = ps_t[:, bset:bset + nbk, :].rearrange("p a b -> p (a b)")
                        R_u = [R_PB[bset + q] for q in range(nbk)]
                        chunk_id = i + 22 * kind
                        cw = cp[:, C_CWF + 3 * chunk_id:C_CWF + 3 * chunk_id + 3]
                        cb = cp[:, C_CBF + chunk_id:C_CBF + chunk_id + 1]
                        ctxs.append((Wt, R_W, bset, u, R_u, cw, cb, ACC[kind], R_ACC[kind]))

                    def e_mm(kind):
                        Wt, R_W, bset, u, R_u, cw, cb, acc, R_acc = ctxs[kind]
                        for ti, (c0, n) in enumerate(tiles):
                            for c in range(8):
                                k.op("pe", lambda e, c=c, ti=ti, c0=c0, n=n: e.matmul(
                                    PB[bset + ti][:, 0:n], lhsT=Wt[:, c, sub * 128:(sub + 1) * 128],
                                    rhs=hT[:, c, t0 + c0:t0 + c0 + n], start=(c == 0), stop=(c == 7)),
                                     reads=[R_W, R_hT], writes=[R_PB[bset + ti]])

                    def e_id(kind):
                        Wt, R_W, bset, u, R_u, cw, cb, acc, R_acc = ctxs[kind]
                        k.op("act", lambda e: e.activation(out=acc[:, 0:ncol], in_=u[:, 0:ncol], func=AF.Identity,
                                                           scale=cw[:, 2:3], bias=cb),
                             reads=R_u + [R_cp], writes=[R_acc])

                    def e_sh(kind):
                        Wt, R_W, bset, u, R_u, cw, cb, acc, R_acc = ctxs[kind]
                        for sh in (1, 2):
                            k.op("dve", lambda e, sh=sh: e.scalar_tensor_tensor(
                                out=acc[:, sh:ncol], in0=u[:, 0:ncol - sh], scalar=cw[:, 2 - sh:3 - sh],
                                in1=acc[:, sh:ncol], op0=ALU.mult, op1=ALU.add),
                                 reads=R_u + [R_cp, R_acc], writes=[R_acc])

                    e_mm(0)
                    e_id(0)
                    e_sh(0)
                    e_mm(1)
                    e_id(1)
                    accg, accv_ = ctxs[0][7], ctxs[1][7]
                    k.op("act", lambda e: e.activation(out=SG, in_=accg[:, halo:halo + 1024], func=AF.Silu),
                         reads=[R_ACC[0]], writes=[R_SG])
                    e_sh(1)
                    k.op("dve", lambda e, i=i: e.tensor_tensor(out=actT[:, i, :], in0=accv_[:, halo:halo + 1024],
                                                               in1=SG, op=ALU.mult),
                         reads=[R_ACC[1], R_SG], writes=[R_actT])
            for bl in range(8):
                b = hf * 8 + bl
                fx, R_fx = FX[b % 2], R_FX[b % 2]
                fy, R_fy = FY[b % 2], R_FY[b % 2]
                k.dma("sp", fx, x2d[b * 128:(b + 1) * 128, :], reads=[R_x2d[b]], writes=[R_fx])
                for half in range(2):
                    pbi = 2 * (b % 4) + half
                    for i in range(22):
                        k.op("pe", lambda e, i=i, half=half, pbi=pbi, bl=bl: e.matmul(
                            PB[pbi], lhsT=actT[:, i, bl * 128:(bl + 1) * 128],
                            rhs=wd[:, i, half * 512:(half + 1) * 512], start=(i == 0), stop=(i == 21)),
                             reads=[R_actT, R_wd], writes=[R_PB[pbi]])
                    k.op("dve", lambda e, half=half, pbi=pbi, fx=fx: e.tensor_tensor(
                        out=fx[:, half * 512:(half + 1) * 512], in0=PB[pbi], in1=fx[:, half * 512:(half + 1) * 512],
                        op=ALU.add), reads=[R_PB[pbi], R_fx], writes=[R_fx])
                k.op("act", lambda e, fx=fx: e.activation(out=fj, in_=fx, func=AF.Square, accum_out=fs[:, 0:1]),
                     reads=[R_fx], writes=[R_fj, R_fs])
                k.op("dve", lambda e: e.tensor_scalar(out=fs[:, 1:2], in0=fs[:, 0:1], scalar1=1.0 / D, scalar2=EPS,
                                                       op0=ALU.mult, op1=ALU.add), reads=[R_fs], writes=[R_fs])
                rsqrt_small(fs[:, 1:2], fs[:, 1:2], R_fs)
                k.op("dve", lambda e, fx=fx, fy=fy: e.scalar_tensor_tensor(
                    out=fy, in0=fx, scalar=fs[:, 1:2], in1=cp[:, C_FG:C_FG + 1024], op0=ALU.mult, op1=ALU.mult),
                     reads=[R_fx, R_fs, R_cp], writes=[R_fy])
                k.dma("pool", out[b * 128:(b + 1) * 128, :], fy, reads=[R_fy], writes=[R_out[oblk % 4]])
                oblk += 1

        for r in R_out:
            if r.dsem is not None:
                k.eng["sp"].wait_ge(k.dsems[r.dsem], r.dcnt)
        for r in R_FY:
            if r.dsem is not None:
                k.eng["sp"].wait_ge(k.dsems[r.dsem], r.dcnt)
    return nc


def _t5_bucket(rel):
    nb = 16
    max_exact = 8
    bucket = np.where(rel > 0, nb, 0)
    n = np.abs(rel)
    nf = np.maximum(n, 1).astype(np.float32)
    large = max_exact + (np.log(nf / np.float32(max_exact)) / np.float32(math.log(128 / max_exact))
                         * np.float32(nb - max_exact)).astype(np.int32)
    large = np.minimum(large, nb - 1)
    return bucket + np.where(n < max_exact, n, large)


def _host_pack(inp):
    f = np.float32
    cp = np.zeros((128, NCP), f)
    cp[:, C_GA:C_GA + 8] = inp["attn_norm_g"][0].reshape(8, 128).T
    cp[:, C_GF:C_GF + 8] = inp["ffn_norm_g"][0].reshape(8, 128).T
    cp[:, C_SUBG] = inp["attn_subln_g"][0]
    for i, nm in enumerate(("lambda_q1", "lambda_k1", "lambda_q2", "lambda_k2")):
        cp[:, C_LAM + 64 * i:C_LAM + 64 * (i + 1)] = inp[nm][0][None, :]
    cp[:, C_CWS:C_CWS + 48] = inp["ssm_conv_w"][0].reshape(4, 12, 128).transpose(2, 1, 0).reshape(128, 48)
    cp[:, C_CBS:C_CBS + 12] = inp["ssm_conv_b"][0].reshape(12, 128).T
    cp[:, C_CWF:C_CWF + 132] = inp["ffn_conv_w"][0].reshape(3, 44, 128).transpose(2, 1, 0).reshape(128, 132)
    cp[:, C_CBF:C_CBF + 44] = inp["ffn_conv_b"][0].reshape(44, 128).T
    cp[:, C_DTB:C_DTB + 16] = inp["ssm_dt_bias"][0][None, :]
    cp[:, C_ALOG:C_ALOG + 16] = inp["ssm_a_log"][0][None, :]
    cp[:, C_D:C_D + 16] = inp["ssm_d"][0][None, :]
    tab = inp["rel_bias_table"]
    cp[:, C_FAR:C_FAR + 8] = tab[15][None, :]
    cp[:, C_FG:C_FG + 1024] = inp["final_norm_g"][None, :]
    cp[:, C_NG:C_NG + 1024] = inp["ssm_norm_g"][0][None, :]
    idx = np.arange(128)
    cp[:, C_ID:C_ID + 128] = np.eye(128, dtype=f)
    cp[:, C_U:C_U + 128] = (idx[:, None] <= idx[None, :]).astype(f)
    cp[:, C_L:C_L + 128] = (idx[:, None] > idx[None, :]).astype(f)
    cp[:, C_ONE:C_ONE + 128] = 1.0
    cp[:, C_DCOL:C_DCOL + 8] = inp["ssm_d"][0][(np.arange(8)[None, :] * 128 + idx[:, None]) // 64]
    cp[:, C_NGCOL:C_NGCOL + 8] = inp["ssm_norm_g"][0].reshape(8, 128).T
    kk = idx[:, None]
    qq = np.arange(384)[None, :]
    rel = kk - qq
    bidx = _t5_bucket(rel)
    bt = tab[bidx]
    bt = np.ascontiguousarray(bt.transpose(0, 2, 1)).astype(f)
    masked = (kk >= 64) & (qq < 64)
    bt[np.broadcast_to(masked[:, None, :], bt.shape)] = -30000.0
    return cp, bt.reshape(128, 8 * 384)


_NC_CACHE = {}


def kernel(**inputs):
    inp = {k_: np.asarray(v) for k_, v in inputs.items()}
    cp, bt = _host_pack(inp)
    if "nc" not in _NC_CACHE:
        _NC_CACHE["nc"] = build_program()
    nc = _NC_CACHE["nc"]
    x = np.ascontiguousarray(inp["x"], dtype=np.float32)
    shared = {
        "w_in": np.ascontiguousarray(inp["w_in"][0]),
        "w_out": np.ascontiguousarray(inp["w_out"][0]),
        "w_up": np.ascontiguousarray(inp["ffn_w_up"][0]),
        "w_down": np.ascontiguousarray(inp["ffn_w_down"][0]),
        "cpack": cp,
        "biasT": bt,
    }
    in_maps = [dict(shared, x=x[i]) for i in range(8)]
    res = run_bass_kernel_spmd(nc, in_maps, core_ids=list(range(8)))
    return np.stack([np.asarray(r["out"], dtype=np.float32) for r in res.results], axis=0)
```

```python
import math
from contextlib import ExitStack

import numpy as np
import concourse.bass as bass
import concourse.mybir as mybir
from concourse.bass_utils import run_bass_kernel_spmd

F32 = mybir.dt.float32
BF16 = mybir.dt.bfloat16
AF = mybir.ActivationFunctionType
ALU = mybir.AluOpType

D = 1024
T = 2048
NB = 16
IN_COLS = 5648
Q0, K0, V0, Z0, XBC0, DT0 = 0, 1024, 2048, 3072, 4096, 5632
FFN = 2816
EPS = 1e-6
SUB_EPS = 1e-5
SSM_EPS = 1e-5
LAM_INIT = 0.8 - 0.6 * math.exp(-0.3 * 0)

C_GA, C_GF, C_SUBG, C_LAM = 0, 8, 16, 17
C_CWS = 273
C_CBS = 321
C_CWF = 333
C_CBF = 465
C_DTB = 509
C_ALOG = 525
C_D = 541
C_FAR = 557
C_FG = 565
C_NG = 1589
C_ID = 2613
C_U = 2741
C_L = 2869
C_ONE = 2997
C_DCOL = 3125
C_NGCOL = 3133
NCP = 3144

WORK_WORDS = 16640


class Res:
    __slots__ = ("name", "w", "r", "dsem", "dcnt", "excl")

    def __init__(self, name, excl=False):
        self.name = name
        self.excl = excl
        self.w = None
        self.r = {}
        self.dsem = None
        self.dcnt = 0


class KB:
    def __init__(self, nc, es):
        self.nc = nc
        self.es = es
        self.eng = dict(pe=nc.tensor, act=nc.scalar, dve=nc.vector, pool=nc.gpsimd, sp=nc.sync)
        self.sem = {k: es.enter_context(nc.semaphore("c_" + k)) for k in ("pe", "act", "dve", "pool")}
        self.cnt = {k: 0 for k in self.sem}
        self.seen = {k: {} for k in self.eng}
        self.nsem = 0
        self.dsems = {}

    def _collect(self, reads, writes, e=None):
        deps = {}

        def add(kind, src, val):
            key = (kind, src)
            if deps.get(key, 0) < val:
                deps[key] = val

        for r in reads:
            if r.w is not None:
                add(*r.w)
            if r.excl:
                for (kind, src), val in r.r.items():
                    if not (kind == "e" and src == e):
                        add(kind, src, val)
        for w in writes:
            if w.w is not None:
                add(*w.w)
            for (kind, src), val in w.r.items():
                add(kind, src, val)
        return deps

    def _wait(self, e, deps):
        for (kind, src), val in deps.items():
            if kind == "e" and src == "pe" and e == "pe":
                continue
            if self.seen[e].get((kind, src), 0) >= val:
                continue
            sem = self.sem[src] if kind == "e" else self.dsems[src]
            self.eng[e].wait_ge(sem, val)
            self.seen[e][(kind, src)] = val

    def op(self, e, fn, reads=(), writes=()):
        self._wait(e, self._collect(reads, writes, e))
        ins = fn(self.eng[e])
        self.cnt[e] += 1
        ins.then_inc(self.sem[e], 1)
        seq = self.cnt[e]
        for r in reads:
            r.r[("e", e)] = seq
        for w in writes:
            w.w = ("e", e, seq)
            w.r = {}
        return ins

    def _dsem(self, r):
        if r.dsem is None:
            name = "d%d" % self.nsem
            self.nsem += 1
            r.dsem = name
            self.dsems[name] = self.es.enter_context(self.nc.semaphore(name))
        return r.dsem

    def dma(self, q, out, in_, reads=(), writes=(), **kw):
        self._wait(q, self._collect(reads, writes))
        ins = self.eng[q].dma_start(out=out, in_=in_, **kw)
        own = writes[0] if writes else reads[0]
        name = self._dsem(own)
        own.dcnt += 16
        ins.then_inc(self.dsems[name], 16)
        for r in reads:
            r.r[("d", name)] = own.dcnt
        for w in writes:
            w.w = ("d", name, own.dcnt)
            w.r = {}
        return ins

    def barrier(self, res_list=()):
        deps = {}
        for k, v in self.cnt.items():
            if v > 0:
                deps[("e", k)] = v
        for r in res_list:
            if r.dsem is not None and r.dcnt > 0:
                deps[("d", r.dsem)] = r.dcnt
        for e in self.eng:
            d = {k: v for k, v in deps.items() if not (k == ("e", "pe") and e == "pe")}
            if e == "pe" and self.cnt["pe"] > 0 and self.seen["pe"].get(("e", "pe"), 0) < self.cnt["pe"]:
                self.eng["pe"].wait_ge(self.sem["pe"], self.cnt["pe"])
                self.seen["pe"][("e", "pe")] = self.cnt["pe"]
            self._wait(e, d)


class Sched:
    LAT = 0.15

    def __init__(self, k, window=160):
        self.k = k
        self.ops = []
        self.lastw = {}
        self.readers = {}
        self.window = window

    DEF_DUR = dict(pe=0.15, act=0.7, dve=0.65, pool=1.5)
    def_scale = 1.0

    def op(self, e, fn, reads=(), writes=(), dur=None, tbl=None):
        if dur is None:
            dur = self.DEF_DUR[e]
        i = len(self.ops)
        preds = set()
        for r in reads:
            if r in self.lastw:
                preds.add(self.lastw[r])
            if r.excl:
                preds |= {p for p in self.readers.get(r, ()) if self.ops[p][0] != e}
        for w in writes:
            if w in self.lastw:
                preds.add(self.lastw[w])
            preds |= self.readers.get(w, set())
        preds.discard(i)
        self.ops.append((e, fn, tuple(reads), tuple(writes), dur, preds, tbl))
        for r in reads:
            self.readers.setdefault(r, set()).add(i)
        for w in writes:
            self.lastw[w] = i
            self.readers[w] = set()

    def flush(self):
        n = len(self.ops)
        fin = [None] * n
        free = {}
        cur_tbl = None
        remaining = list(range(n))
        while remaining:
            best = None
            best_st = None
            for i in remaining[:self.window]:
                e, fn, rd, wr, dur, preds, tbl = self.ops[i]
                ok = True
                st = free.get(e, 0.0)
                for p in preds:
                    if fin[p] is None:
                        ok = False
                        break
                    lat = 0.0 if (self.ops[p][0] == e) else self.LAT
                    st = max(st, fin[p] + lat)
                if not ok:
                    continue
                if tbl is not None and cur_tbl is not None and tbl != cur_tbl:
                    st += 1.3
                if best is None or st < best_st - 1e-9:
                    best, best_st = i, st
            e, fn, rd, wr, dur, preds, tbl = self.ops[best]
            if tbl is not None:
                cur_tbl = tbl
            fin[best] = best_st + dur
            free[e] = fin[best]
            remaining.remove(best)
            self.k.op(e, fn, reads=rd, writes=wr)
        self.ops = []
        self.lastw = {}
        self.readers = {}
        return max(free.values()) if free else 0.0


def apx(base, dims):
    return bass.AP(base.tensor, base.offset, [list(base.ap[0])] + [list(d) for d in dims])


class Carver:
    def __init__(self, work_ap):
        self.w = work_ap
        self.pos = 0

    def f32(self, n):
        a = self.w[:, self.pos:self.pos + n]
        self.pos += n
        assert self.pos <= WORK_WORDS, self.pos
        return a

    def bf16(self, n):
        assert n % 2 == 0
        return self.f32(n // 2).bitcast(BF16)


def build_program(dbg=None):
    nc = bass.Bass("TRN2", target_bir_lowering=False)
    x = nc.dram_tensor("x", [T, D], F32, kind="ExternalInput").ap()
    w_in = nc.dram_tensor("w_in", [D, IN_COLS], F32, kind="ExternalInput").ap()
    w_out = nc.dram_tensor("w_out", [2048, D], F32, kind="ExternalInput").ap()
    w_up = nc.dram_tensor("w_up", [D, 2 * FFN], F32, kind="ExternalInput").ap()
    w_down = nc.dram_tensor("w_down", [FFN, D], F32, kind="ExternalInput").ap()
    cpack_d = nc.dram_tensor("cpack", [128, NCP], F32, kind="ExternalInput").ap()
    bias_d = nc.dram_tensor("biasT", [128, 8 * 384], F32, kind="ExternalInput").ap()
    out = nc.dram_tensor("out", [T, D], F32, kind="ExternalOutput").ap()
    x2d = nc.dram_tensor("x2s", [T, D], F32, kind="Internal").ap()
    dbg_out = None
    if dbg is not None:
        dbg_out = nc.dram_tensor("dbg", [128, 16 * 2048], F32, kind="ExternalOutput").ap()

    with ExitStack() as es:
        hT_t = es.enter_context(nc.sbuf_tensor("hT", [128, 8, T], BF16))
        big_t = es.enter_context(nc.sbuf_tensor("big", [128, 45056], BF16))
        work_t = es.enter_context(nc.sbuf_tensor("work", [128, WORK_WORDS], F32))
        cp_t = es.enter_context(nc.sbuf_tensor("cp", [128, NCP], F32))
        bias_t = es.enter_context(nc.sbuf_tensor("biasbf", [128, 8, 384], BF16))
        cst_t = es.enter_context(nc.sbuf_tensor("cst", [128, 512], BF16))
        sm_t = es.enter_context(nc.sbuf_tensor("sm", [128, 128], F32))
        ps_t = es.enter_context(nc.psum_tensor("ps", [128, 8, 512], F32))
        k = KB(nc, es)
        es.enter_context(nc.Block())

        hT = hT_t[:]
        big = big_t[:]
        work = work_t[:]
        cp = cp_t[:]
        biasb = bias_t[:]
        sm = sm_t[:]
        ident_bf = cst_t[:, 0:128]
        ones_bf = cst_t[:, 128:256]
        mixT = big[:, 0:32768].rearrange("p (c t) -> p c t", c=16)
        wo0 = big[:, 32768:32768 + 8192].rearrange("p (c n) -> p c n", c=16)
        PB = [ps_t[:, i, :] for i in range(8)]
        PBbf = [ps_t[:, i, :].bitcast(BF16) for i in range(8)]

        R_hT = Res("hT")
        R_mix = Res("mixT")
        R_big2 = Res("big2")
        R_cp = Res("cp")
        R_bias = Res("bias")
        R_cst = Res("cst")
        R_sm = Res("sm")
        R_PB = [Res("pb%d" % i, excl=True) for i in range(8)]
        R_x2d = [Res("x2d%d" % i) for i in range(NB)]
        R_out = [Res("out%d" % i) for i in range(4)]
        R_dbg = Res("dbg")
        dma_res = []

        def wres(name):
            r = Res(name)
            dma_res.append(r)
            return r

        k.dma("sp", cp, cpack_d[:, :], writes=[R_cp])
        for hh in range(2):
            k.dma("pool", biasb[:, 4 * hh:4 * hh + 4, :],
                  bias_d[:, 1536 * hh:1536 * (hh + 1)].rearrange("p (h n) -> p h n", h=4), writes=[R_bias])
        k.op("dve", lambda e: e.tensor_copy(out=ident_bf, in_=cp[:, C_ID:C_ID + 128]), reads=[R_cp], writes=[R_cst])
        k.op("dve", lambda e: e.tensor_copy(out=ones_bf, in_=cp[:, C_ONE:C_ONE + 128]), reads=[R_cp], writes=[R_cst])
        U_f = cp[:, C_U:C_U + 128]
        L_f = cp[:, C_L:C_L + 128]
        ones_f = cp[:, C_ONE:C_ONE + 128]
        NEGLAM = sm[:, 0:1]
        SUBG8 = sm[:, 1:2]
        NEGA = sm[:, 8:24]
        junk64 = sm[:, 32:64]
        k.op("dve", lambda e: e.memset(sm[:, 0:8], 0.0), writes=[R_sm])
        lq = sm[:, 64:128]
        R_tmp0 = Res("tmp0")
        for i in range(2):
            a = cp[:, C_LAM + 128 * i:C_LAM + 128 * i + 64]
            b = cp[:, C_LAM + 128 * i + 64:C_LAM + 128 * i + 128]
            k.op("dve", lambda e, a=a, b=b: e.tensor_tensor(out=lq, in0=a, in1=b, op=ALU.mult),
                 reads=[R_cp], writes=[R_tmp0])
            k.op("dve", lambda e, i=i: e.tensor_reduce(out=sm[:, 2 + i:3 + i], in_=lq, axis=mybir.AxisListType.X,
                                                        op=ALU.add), reads=[R_tmp0], writes=[R_sm])
        k.op("act", lambda e: e.activation(out=sm[:, 4:6], in_=sm[:, 2:4], func=AF.Exp), reads=[R_sm], writes=[R_sm])
        k.op("dve", lambda e: e.tensor_tensor(out=sm[:, 0:1], in0=sm[:, 5:6], in1=sm[:, 4:5], op=ALU.subtract),
             reads=[R_sm], writes=[R_sm])
        k.op("dve", lambda e: e.tensor_scalar(out=sm[:, 0:1], in0=sm[:, 0:1], scalar1=-LAM_INIT, scalar2=None,
                                               op0=ALU.add), reads=[R_sm], writes=[R_sm])
        k.op("dve", lambda e: e.tensor_scalar(out=SUBG8, in0=cp[:, C_SUBG:C_SUBG + 1], scalar1=1.0 - LAM_INIT,
                                               scalar2=None, op0=ALU.mult), reads=[R_cp, R_sm], writes=[R_sm])
        k.op("act", lambda e: e.activation(out=NEGA, in_=cp[:, C_ALOG:C_ALOG + 16], func=AF.Exp),
             reads=[R_cp, R_sm], writes=[R_sm])
        k.op("dve", lambda e: e.tensor_scalar(out=NEGA, in0=NEGA, scalar1=-1.0, scalar2=None, op0=ALU.mult),
             reads=[R_sm], writes=[R_sm])

        def rsqrt_small(v_ap, out_ap, res, em=None):
            em = em or k
            kw = dict(dur=0.2, tbl="E") if em is not k else {}
            kw2 = dict(dur=0.2, tbl="E") if em is not k else {}
            em.op("act", lambda e: e.activation(out=out_ap, in_=v_ap, func=AF.Ln), reads=[res], writes=[res], **kw)
            em.op("act", lambda e: e.activation(out=out_ap, in_=out_ap, func=AF.Exp, scale=-0.5), reads=[res],
                  writes=[res], **kw2)

        def load_w(dst, src_rows_ap, res, q="pool"):
            k.dma(q, dst, src_rows_ap.rearrange("(c p) n -> p c n", p=128), writes=[res])

        def norm_stats(src_ap, src_res, cv):
            ss, v, rstd, junk, hb, R_s, R_j, R_hb = cv
            k.op("act", lambda e: e.activation(out=junk, in_=src_ap, func=AF.Square, accum_out=ss),
                 reads=[src_res], writes=[R_j, R_s])
            k.op("dve", lambda e: e.tensor_scalar(out=v, in0=ss, scalar1=1.0 / D, scalar2=EPS, op0=ALU.mult,
                                                   op1=ALU.add), reads=[R_s], writes=[R_s])

        def norm_block_to_hT(src_ap, src_res, blk, gcol0, pbi, cv, stats_done=False):
            ss, v, rstd, junk, hb, R_s, R_j, R_hb = cv
            if not stats_done:
                norm_stats(src_ap, src_res, cv)
            rsqrt_small(v, rstd, R_s)
            k.op("dve", lambda e: e.tensor_scalar(out=hb, in0=src_ap, scalar1=rstd, scalar2=None, op0=ALU.mult),
                 reads=[src_res, R_s], writes=[R_hb])
            pv = PBbf[pbi].rearrange("p (c t) -> p c t", c=8)
            for c in range(8):
                k.op("pe", lambda e, c=c: e.transpose(out=pv[:, c, :], in_=hb[:, c * 128:(c + 1) * 128],
                                                      identity=ident_bf),
                     reads=[R_hb, R_cst], writes=[R_PB[pbi]])
            gb = apx(cp[:, gcol0:gcol0 + 1], [[1, 8], [0, 128]])
            k.op("dve", lambda e: e.tensor_tensor(out=hT[:, :, blk * 128:(blk + 1) * 128], in0=pv, in1=gb,
                                                  op=ALU.mult),
                 reads=[R_PB[pbi], R_cp], writes=[R_hT])

        cv = Carver(work)
        XB = [cv.f32(1024) for _ in range(3)]
        R_XB = [wres("xb%d" % i) for i in range(3)]
        A_sets = []
        for i in range(2):
            s = cv.f32(4)
            A_sets.append((s[:, 0:1], s[:, 1:2], s[:, 2:3], cv.bf16(1024), cv.bf16(1024), Res("a_s%d" % i),
                           Res("a_j%d" % i), Res("a_hb%d" % i)))
        k.dma("sp", XB[0], x[0:128, :], writes=[R_XB[0]])
        norm_stats(XB[0], R_XB[0], A_sets[0])
        for b in range(NB):
            if b + 1 < NB:
                k.dma("sp", XB[(b + 1) % 3], x[(b + 1) * 128:(b + 2) * 128, :], writes=[R_XB[(b + 1) % 3]])
                norm_stats(XB[(b + 1) % 3], R_XB[(b + 1) % 3], A_sets[(b + 1) % 2])
            norm_block_to_hT(XB[b % 3], R_XB[b % 3], b, C_GA, b % 2, A_sets[b % 2], stats_done=True)

        if dbg == "A":
            k.barrier(dma_res)
            cvd = Carver(work)
            t = cvd.f32(8 * 2048 // 2 * 2)
            k.op("dve", lambda e: e.tensor_copy(out=work[:, 0:16384], in_=hT.rearrange("p c t -> p (c t)")),
                 reads=[R_hT], writes=[R_dbg])
            k.dma("sp", dbg_out[:, 0:16384], work[:, 0:16384], reads=[R_dbg])
            k.eng["sp"].wait_ge(k.dsems[R_dbg.dsem], R_dbg.dcnt)
            return nc

        k.barrier(dma_res)
        cv = Carver(work)
        Wq = cv.bf16(2048).rearrange("p (c n) -> p c n", c=8)
        Wk = cv.bf16(2048).rearrange("p (c n) -> p c n", c=8)
        Wv = cv.bf16(2048).rearrange("p (c n) -> p c n", c=8)
        R_Wq, R_Wk, R_Wv = wres("wq"), wres("wk"), wres("wv")
        QT2 = [cv.bf16(2048) for _ in range(2)]
        KT2 = [cv.bf16(4096).rearrange("p (m t) -> p m t", m=2) for _ in range(2)]
        V2 = [cv.bf16(16 * 130).rearrange("p (b e) -> p b e", b=16) for _ in range(2)]
        R_QT2 = [Res("qt0"), Res("qt1")]
        R_KT2 = [Res("kt0"), Res("kt1")]
        R_V2 = [Res("v0"), Res("v1")]
        PT = [cv.bf16(512).rearrange("p (m q) -> p m q", m=2) for _ in range(3)]
        R_PT = [Res("pt%d" % i) for i in range(3)]
        a_all = cv.f32(2048).rearrange("p (j e) -> p j e", j=16)
        R_aall = Res("a_all")
        y_all = cv.bf16(2048).rearrange("p (j e) -> p j e", j=16)
        R_yall = Res("y_all")
        sml = cv.f32(64)
        ss_all = sml[:, 0:16]
        rstd_all = sml[:, 16:32]
        R_ssall = Res("ss_all")
        RR = [sml[:, 32 + 4 * i:36 + 4 * i] for i in range(4)]
        R_RR = [Res("rr%d" % i) for i in range(4)]
        A1 = [cv.f32(128) for _ in range(2)]
        R_A1 = [Res("a1_%d" % i) for i in range(2)]
        jk = cv.f32(128)
        R_jk = Res("jk")

        for i in range(2):
            k.op("dve", lambda e, i=i: e.memset(V2[i][:, :, 128:130], 1.0), writes=[R_V2[i]])
            k.op("dve", lambda e, i=i: e.memset(KT2[i][64:128, 0, :], 0.0), writes=[R_KT2[i]])
            k.op("dve", lambda e, i=i: e.memset(KT2[i][0:64, 1, :], 0.0), writes=[R_KT2[i]])

        evac_i = [0]

        def evac(out_ap, in_ap, reads, writes, scale=None):
            evac_i[0] += 1
            if evac_i[0] % 2 == 0:
                if scale is None:
                    k.op("act", lambda e: e.activation(out=out_ap, in_=in_ap, func=AF.Copy), reads=reads,
                         writes=writes)
                else:
                    k.op("act", lambda e: e.activation(out=out_ap, in_=in_ap, func=AF.Copy, scale=scale),
                         reads=reads, writes=writes)
            else:
                if scale is None:
                    k.op("dve", lambda e: e.tensor_copy(out=out_ap, in_=in_ap), reads=reads, writes=writes)
                else:
                    k.op("dve", lambda e: e.tensor_scalar(out=out_ap, in0=in_ap, scalar1=scale, scalar2=None,
                                                          op0=ALU.mult), reads=reads, writes=writes)

        n_heads = 8
        if dbg == "B1":
            n_heads = 2
        if dbg in ("C", "C1"):
            n_heads = 0

        def load_pair(pair):
            load_w(Wq, w_in[:, Q0 + pair * 256:Q0 + (pair + 1) * 256], R_Wq)
            load_w(Wk, w_in[:, K0 + pair * 256:K0 + (pair + 1) * 256], R_Wk)
            load_w(Wv, w_in[:, V0 + pair * 256:V0 + (pair + 1) * 256], R_Wv)

        pj_i = [0]

        def evac_dve(out_ap, in_ap, reads, writes, scale=None):
            if scale is None:
                k.op("dve", lambda e: e.tensor_copy(out=out_ap, in_=in_ap), reads=reads, writes=writes)
            else:
                k.op("dve", lambda e: e.tensor_scalar(out=out_ap, in0=in_ap, scalar1=scale, scalar2=None,
                                                      op0=ALU.mult), reads=reads, writes=writes)

        def proj_units(h):
            hh = h % 2
            buf = h % 2
            mops = []
            for kind in ("q", "k"):
                for tt in range(4):
                    pbi = 6 + pj_i[0] % 2
                    pj_i[0] += 1
                    Wt, R_W = (Wq, R_Wq) if kind == "q" else (Wk, R_Wk)
                    for c in range(8):
                        mops.append((0.22, lambda c=c, pbi=pbi, Wt=Wt, R_W=R_W, tt=tt: k.op(
                            "pe", lambda e: e.matmul(PB[pbi], lhsT=Wt[:, c, hh * 128:(hh + 1) * 128],
                                                     rhs=hT[:, c, tt * 512:(tt + 1) * 512], start=(c == 0),
                                                     stop=(c == 7)),
                            reads=[R_W, R_hT], writes=[R_PB[pbi]])))
                    if kind == "q":
                        mops.append((0.0, lambda pbi=pbi, tt=tt: evac_dve(
                            QT2[buf][:, tt * 512:(tt + 1) * 512], PB[pbi], [R_PB[pbi]], [R_QT2[buf]], scale=0.125)))
                    else:
                        for m in range(2):
                            mops.append((0.0, lambda pbi=pbi, tt=tt, m=m: evac_dve(
                                KT2[buf][m * 64:(m + 1) * 64, m, tt * 512:(tt + 1) * 512],
                                PB[pbi][m * 64:(m + 1) * 64, :], [R_PB[pbi]], [R_KT2[buf]])))
            for b in range(NB):
                pbi = 6 + pj_i[0] % 2
                pj_i[0] += 1
                for c in range(8):
                    mops.append((0.06, lambda c=c, pbi=pbi, b=b: k.op(
                        "pe", lambda e: e.matmul(PB[pbi][:, 0:128], lhsT=hT[:, c, b * 128:(b + 1) * 128],
                                                 rhs=Wv[:, c, hh * 128:(hh + 1) * 128], start=(c == 0),
                                                 stop=(c == 7)),
                        reads=[R_Wv, R_hT], writes=[R_PB[pbi]])))
                mops.append((0.0, lambda pbi=pbi, b=b: evac_dve(V2[buf][:, b, 0:128], PB[pbi][:, 0:128],
                                                                [R_PB[pbi]], [R_V2[buf]])))
            if hh == 1 and h + 1 < n_heads:
                mops.append((0.0, lambda: load_pair((h + 1) // 2)))
            return mops

        tiles = [(h, J, kb) for h in range(n_heads) for J in range(8) for kb in range(2 * J + 2)]

        def tile_ctx(n):
            h, J, kb = tiles[n]
            q0 = 128 if kb == 2 * J + 1 else 0
            sti = n % 2
            st = PB[sti].rearrange("p (m q) -> p m q", m=2)
            return h, J, kb, q0, sti, st, PT[n % 3], R_PT[n % 3]

        def stage1(n):
            h, J, kb, q0, sti, st, pt, R_pt = tile_ctx(n)
            buf = h % 2
            QT, KT = QT2[buf], KT2[buf]
            far_b = cp[:, C_FAR + h:C_FAR + h + 1]
            near = kb >= 2 * J - 1
            if near:
                if kb == 2 * J - 1:
                    bsl = biasb[:, h, 128:384]
                elif kb == 2 * J:
                    bsl = biasb[:, h, 0:256]
                else:
                    bsl = biasb[:, h, 0:128]
            for m in range(2):
                k.op("pe", lambda e, m=m: e.matmul(
                    st[:, m, q0:256], lhsT=KT[:, m, kb * 128:(kb + 1) * 128],
                    rhs=QT[:, J * 256 + q0:(J + 1) * 256], start=True, stop=not near,
                    skip_group_check=True), reads=[R_KT2[buf], R_QT2[buf]], writes=[R_PB[sti]])
                if near:
                    k.op("pe", lambda e, m=m: e.matmul(
                        st[:, m, q0:256], lhsT=ident_bf, rhs=bsl, start=False, stop=True,
                        skip_group_check=True), reads=[R_cst, R_bias], writes=[R_PB[sti]])
            if near:
                k.op("act", lambda e: e.activation(out=pt[:, :, q0:256], in_=st[:, :, q0:256], func=AF.Exp),
                     reads=[R_PB[sti]], writes=[R_pt])
            else:
                k.op("act", lambda e: e.activation(out=pt[:, :, :], in_=st[:, :, :], func=AF.Exp, bias=far_b),
                     reads=[R_PB[sti], R_cp], writes=[R_pt])

        def stage2(n):
            h, J, kb, q0, sti, st, pt, R_pt = tile_ctx(n)
            buf = h % 2
            Vt = V2[buf]
            accb = [2 + 2 * (J % 2), 3 + 2 * (J % 2)]
            accv = [PB[i][:, 0:260].rearrange("p (m e) -> p m e", m=2) for i in accb]
            for jj in range(2):
                if jj * 128 < q0:
                    continue
                last_kb = 2 * J + jj
                for m in range(2):
                    k.op("pe", lambda e, jj=jj, m=m: e.matmul(
                        accv[jj][:, m, 0:129], lhsT=pt[:, m, jj * 128:(jj + 1) * 128],
                        rhs=Vt[:, kb, 0:129], start=(kb == 0 and m == 0), stop=(kb == last_kb),
                        skip_group_check=True), reads=[R_pt, R_V2[buf]], writes=[R_PB[accb[jj]]])
                if kb == last_kb:
                    jb = 2 * J + jj
                    acc = accv[jj]
                    R_acc = R_PB[accb[jj]]
                    rr, R_rr = RR[jb % 4], R_RR[jb % 4]
                    a1, R_a1 = A1[jb % 2], R_A1[jb % 2]
                    k.op("dve", lambda e, acc=acc, rr=rr: e.reciprocal(out=rr[:, 0:2], in_=acc[:, :, 128]),
                         reads=[R_acc], writes=[R_rr])
                    k.op("dve", lambda e, rr=rr: e.tensor_tensor(out=rr[:, 2:3], in0=rr[:, 1:2], in1=NEGLAM,
                                                                 op=ALU.mult),
                         reads=[R_rr, R_sm], writes=[R_rr])
                    k.op("dve", lambda e, acc=acc, rr=rr, a1=a1: e.tensor_scalar(
                        out=a1, in0=acc[:, 0, 0:128], scalar1=rr[:, 0:1], scalar2=None, op0=ALU.mult),
                         reads=[R_acc, R_rr], writes=[R_a1])
                    k.op("dve", lambda e, acc=acc, rr=rr, a1=a1, jb=jb: e.scalar_tensor_tensor(
                        out=a_all[:, jb, :], in0=acc[:, 1, 0:128], scalar=rr[:, 2:3], in1=a1,
                        op0=ALU.mult, op1=ALU.add), reads=[R_acc, R_rr, R_a1], writes=[R_aall])
                    k.op("dve", lambda e, jb=jb: e.tensor_tensor(out=jk, in0=a_all[:, jb, :],
                                                                 in1=a_all[:, jb, :], op=ALU.mult),
                         reads=[R_aall], writes=[R_jk])
                    k.op("dve", lambda e, jb=jb: e.tensor_reduce(out=ss_all[:, jb:jb + 1], in_=jk,
                                                                 axis=mybir.AxisListType.X, op=ALU.add),
                         reads=[R_jk], writes=[R_ssall])
            if J == 7 and kb == 15:
                k.op("dve", lambda e: e.tensor_scalar(out=rstd_all, in0=ss_all, scalar1=1.0 / 128,
                                                       scalar2=SUB_EPS, op0=ALU.mult, op1=ALU.add),
                     reads=[R_ssall], writes=[R_ssall])
                rsqrt_small(rstd_all, rstd_all, R_ssall)
                rb = apx(rstd_all[:, 0:1], [[1, 16], [0, 128]])
                k.op("dve", lambda e: e.tensor_tensor(out=y_all, in0=a_all, in1=rb, op=ALU.mult),
                     reads=[R_aall, R_ssall], writes=[R_yall])
                def epi_pe(h=h):
                    for half in range(2):
                        pbi = 6 + half
                        pv = PBbf[pbi].rearrange("p (j q) -> p j q", j=8)
                        for j8 in range(8):
                            k.op("pe", lambda e, pv=pv, j8=j8, half=half: e.transpose(
                                out=pv[:, j8, :], in_=y_all[:, half * 8 + j8, :], identity=ident_bf),
                                reads=[R_yall, R_cst], writes=[R_PB[pbi]])
                        k.op("dve", lambda e, half=half, pbi=pbi: e.tensor_scalar(
                            out=mixT[:, h, half * 1024:(half + 1) * 1024], in0=PBbf[pbi], scalar1=SUBG8,
                            scalar2=None, op0=ALU.mult), reads=[R_PB[pbi], R_sm], writes=[R_mix])
                if h + 1 < n_heads:
                    epi_pending.append(epi_pe)
                else:
                    epi_pe()

        deferred = []
        epi_pending = []
        if n_heads > 0:
            load_pair(0)
            for _, u in proj_units(0):
                u()
            pend = []
            for n in range(len(tiles)):
                h, J, kb = tiles[n]
                if J == 0 and kb == 0:
                    pend = proj_units(h + 1) if h + 1 < n_heads else []
                    if epi_pending:
                        pend = [(0.9, epi_pending.pop())] + pend
                    total_cost = sum(c for c, _ in pend)
                    rate = total_cost / 62.0
                    budget = 0.0
                    done_u = 0
                if n == 0:
                    stage1(0)
                if n + 1 < len(tiles):
                    stage1(n + 1)
                budget += rate
                while done_u < len(pend) and (pend[done_u][0] <= budget or (J == 7 and kb >= 13)):
                    budget -= pend[done_u][0]
                    pend[done_u][1]()
                    done_u += 1
                stage2(n)

        if dbg in ("B", "B1"):
            k.barrier(dma_res)
            k.op("dve", lambda e: e.tensor_copy(out=work[:, 0:16384], in_=big[:, 0:16384]),
                 reads=[R_mix], writes=[R_dbg])
            k.dma("sp", dbg_out[:, 0:16384], work[:, 0:16384], reads=[R_dbg])
            k.eng["sp"].wait_ge(k.dsems[R_dbg.dsem], R_dbg.dcnt)
            return nc

        k.barrier(dma_res)
        cv = Carver(work)
        xsT = cv.bf16(8192).rearrange("p (c t) -> p c t", c=4)
        bcT = cv.bf16(4096).rearrange("p (c t) -> p c t", c=2)
        R_xsT, R_bcT = Res("xsT"), Res("bcT")
        Wxz = cv.bf16(4096).rearrange("p (c n) -> p c n", c=8)
        Wdt = cv.bf16(128).rearrange("p (c n) -> p c n", c=8)
        R_Wxz, R_Wbc, R_Wdt = wres("wxz"), wres("wbc"), wres("wdt")
        dt_all = cv.f32(256).rearrange("p (b h) -> p b h", b=16)
        edc_all = cv.f32(768).rearrange("p (b h) -> p b h", b=16)
        ahi_all = cv.bf16(256).rearrange("p (b h) -> p b h", b=16)
        alo_all = cv.bf16(256).rearrange("p (b h) -> p b h", b=16)
        R_dt = Res("dt_all")
        R_edc = Res("edc_all")
        R_ahl = Res("ahl")
        temp0 = cv.pos
        acc_c2 = [cv.f32(2048), cv.f32(2048)]
        Wbc = cv.bf16(2048).rearrange("p (c n) -> p c n", c=8)
        R_accc2 = [Res("acc_c0"), Res("acc_c1")]
        cv.pos = temp0
        rhsH = cv.bf16(1024)
        t1_alt = cv.f32(512)
        dec = cv.bf16(1024)
        MT2 = [cv.bf16(1024) for _ in range(2)]
        xdt2 = [cv.bf16(512) for _ in range(2)]
        xdd2 = [cv.bf16(512) for _ in range(2)]
        bmsb2 = [cv.bf16(128) for _ in range(2)]
        t1 = cv.f32(512)
        t1b = cv.bf16(512)
        R_t1b = Res("t1b")
        diagD = [cv.bf16(128) for _ in range(4)]
        sz = cv.f32(512)
        sz_alt = cv.f32(512)
        o_tm = cv.bf16(512)
        S_f = cv.f32(512)
        S_b = cv.bf16(512)
        cbm = cv.bf16(128)
        smc = cv.f32(48)
        R_rhs, R_dec = Res("rhsHL"), Res("dec")
        R_MT2 = [Res("MT%d" % i) for i in range(2)]
        R_xdt2 = [Res("xdt%d" % i) for i in range(2)]
        R_xdd2 = [Res("xdd%d" % i) for i in range(2)]
        R_bmsb2 = [Res("bmsb%d" % i) for i in range(2)]
        R_t1, R_dD, R_sz, R_otm = Res("t1"), Res("diagD"), Res("sz"), Res("otm")
        t1_2 = [t1, t1_alt]
        sz_2 = [sz, sz_alt]
        R_sz_2 = [R_sz, Res("sz2")]
        R_t1_2 = [R_t1, Res("t1b2")]
        R_S, R_Sb, R_cbm, R_smc = Res("S"), Res("Sb"), Res("cbm"), Res("smc")
        U_bf = cst_t[:, 256:384]
        L_bf = cst_t[:, 384:512]
        k.op("dve", lambda e: e.tensor_copy(out=U_bf, in_=U_f), reads=[R_cp], writes=[R_cst])
        k.op("dve", lambda e: e.tensor_copy(out=L_bf, in_=L_f), reads=[R_cp], writes=[R_cst])

        k.dma("pool", wo0, w_out[:, 0:512].rearrange("(c p) n -> p c n", p=128), writes=[R_big2])

        load_w(Wdt, w_in[:, DT0:DT0 + 16], R_Wdt)
        sc0 = Sched(k)
        for b in range(NB):
            pbi = b % 2
            dtb = smc[:, 16 * pbi:16 * pbi + 16]
            al = sm[:, 96 + 16 * pbi:112 + 16 * pbi]
            R_al = R_smc
            for c in range(8):
                sc0.op("pe", lambda e, c=c, b=b, pbi=pbi: e.matmul(PB[pbi][:, 0:16],
                                                                  lhsT=hT[:, c, b * 128:(b + 1) * 128],
                                                                  rhs=Wdt[:, c, :], start=(c == 0), stop=(c == 7)),
                     reads=[R_Wdt, R_hT], writes=[R_PB[pbi]])
            sc0.op("dve", lambda e, pbi=pbi, dtb=dtb: e.tensor_tensor(out=dtb, in0=PB[pbi][:, 0:16],
                                                                     in1=cp[:, C_DTB:C_DTB + 16], op=ALU.add),
                 reads=[R_PB[pbi], R_cp], writes=[R_smc])
            sc0.op("act", lambda e, dtb=dtb: e.activation(out=dtb, in_=dtb, func=AF.Exp), reads=[R_smc],
                 writes=[R_smc])
            sc0.op("act", lambda e, b=b, dtb=dtb: e.activation(out=dt_all[:, b, :], in_=dtb, func=AF.Ln, bias=1.0),
                 reads=[R_smc], writes=[R_dt])
            sc0.op("dve", lambda e, b=b, al=al: e.tensor_tensor(out=al, in0=dt_all[:, b, :], in1=NEGA, op=ALU.mult),
                 reads=[R_dt, R_sm], writes=[R_al])
            for i, M_ in enumerate((U_f, L_f, ones_f)):
                sc0.op("pe", lambda e, i=i, M_=M_, al=al, pbi=pbi: e.matmul(PB[2 + pbi][:, 16 * i:16 * i + 16],
                                                                          lhsT=M_, rhs=al, start=True, stop=True),
                     reads=[R_al, R_cp], writes=[R_PB[2 + pbi]])
            sc0.op("act", lambda e, b=b, pbi=pbi: e.activation(out=edc_all[:, b, :], in_=PB[2 + pbi][:, 0:48],
                                                             func=AF.Exp), reads=[R_PB[2 + pbi]], writes=[R_edc])
            sc0.op("dve", lambda e, b=b, al=al: e.tensor_copy(out=ahi_all[:, b, :], in_=al), reads=[R_al],
                 writes=[R_ahl])
            sc0.op("dve", lambda e, b=b, al=al: e.tensor_tensor(out=alo_all[:, b, :], in0=al, in1=ahi_all[:, b, :],
                                                              op=ALU.subtract), reads=[R_al, R_ahl],
                 writes=[R_ahl])

        sc0.flush()
        n_groups = 2 if dbg != "C1" else 1
        for g in range(n_groups):
            load_w(Wxz, w_in[:, XBC0 + g * 512:XBC0 + (g + 1) * 512], R_Wxz)
            load_w(Wbc[:, :, 0:128], w_in[:, XBC0 + 1024 + g * 128:XBC0 + 1024 + (g + 1) * 128], R_Wbc)
            load_w(Wbc[:, :, 128:256], w_in[:, XBC0 + 1280 + g * 128:XBC0 + 1280 + (g + 1) * 128], R_Wbc)
            sc1 = Sched(k)
            for ci in range(6):
                if ci < 4:
                    Wt, R_W, cs, chunk_id = Wxz, R_Wxz, ci * 128, g * 4 + ci
                    dst, R_dst = xsT[:, ci, :], R_xsT
                else:
                    Wt, R_W, cs, chunk_id = Wbc, R_Wbc, (ci - 4) * 128, 8 + 2 * (ci - 4) + g
                    dst, R_dst = bcT[:, ci - 4, :], R_bcT
                bset = 4 * (ci % 2)
                acc_c, R_accc = acc_c2[ci % 2], R_accc2[ci % 2]
                u = ps_t[:, bset:bset + 4, :].rearrange("p a b -> p (a b)")
                R_u = [R_PB[bset + i] for i in range(4)]
                for tt in range(4):
                    for c in range(8):
                        sc1.op("pe", lambda e, c=c, tt=tt, Wt=Wt, cs=cs, bset=bset: e.matmul(
                            PB[bset + tt], lhsT=Wt[:, c, cs:cs + 128], rhs=hT[:, c, tt * 512:(tt + 1) * 512],
                            start=(c == 0), stop=(c == 7)), reads=[R_W, R_hT], writes=[R_PB[bset + tt]], dur=0.22)
                cw = cp[:, C_CWS + 4 * chunk_id:C_CWS + 4 * chunk_id + 4]
                cb = cp[:, C_CBS + chunk_id:C_CBS + chunk_id + 1]
                sc1.op("act", lambda e, u=u, cw=cw, cb=cb, acc_c=acc_c: e.activation(out=acc_c, in_=u, func=AF.Identity,
                                                                                   scale=cw[:, 3:4], bias=cb),
                     reads=R_u + [R_cp], writes=[R_accc], dur=2.0)
                for sh in (1, 2, 3):
                    sc1.op("dve", lambda e, u=u, cw=cw, sh=sh, acc_c=acc_c: e.scalar_tensor_tensor(
                        out=acc_c[:, sh:T], in0=u[:, 0:T - sh], scalar=cw[:, 3 - sh:4 - sh], in1=acc_c[:, sh:T],
                        op0=ALU.mult, op1=ALU.add), reads=R_u + [R_cp, R_accc], writes=[R_accc], dur=2.35)
                sc1.op("act", lambda e, dst=dst, acc_c=acc_c: e.activation(out=dst, in_=acc_c, func=AF.Silu),
                     reads=[R_accc], writes=[R_dst], dur=1.9)
            sc1.flush()
            load_w(Wxz, w_in[:, Z0 + g * 512:Z0 + (g + 1) * 512], R_Wxz)
            k.barrier([R_Wbc])
            for i in range(4):
                k.op("dve", lambda e, i=i: e.tensor_scalar(out=diagD[i], in0=ident_bf,
                                                           scalar1=cp[:, C_DCOL + g * 4 + i:C_DCOL + g * 4 + i + 1],
                                                           scalar2=None, op0=ALU.mult),
                     reads=[R_cst, R_cp], writes=[R_dD])
            k.op("dve", lambda e: e.memset(S_f, 0.0), writes=[R_S])
            k.op("dve", lambda e: e.memset(S_b, 0.0), writes=[R_Sb])

            sch = Sched(k)

            def front(b):
                blk = slice(b * 128, (b + 1) * 128)
                p2 = b % 2
                MT, xdt, xdd, bm_sb = MT2[p2], xdt2[p2], xdd2[p2], bmsb2[p2]
                dt_b = dt_all[:, b, g * 8:(g + 1) * 8]
                dte = edc_all[:, b, 16 + g * 8:24 + g * 8]
                for dst, src in ((rhsH, ahi_all),):
                    sch.op("dve", lambda e, dst=dst, src=src: e.tensor_tensor(
                        out=dst.rearrange("p (r l) -> p r l", r=8), in0=apx(U_bf[:, 0:1], [[0, 8], [1, 128]]),
                        in1=apx(src[:, b, g * 8:g * 8 + 1], [[1, 8], [0, 128]]), op=ALU.mult),
                         reads=[R_cst, R_ahl], writes=[R_rhs], dur=1.23)
                for hf in range(2):
                    sch.op("pe", lambda e, hf=hf: e.matmul(PB[3 + hf], lhsT=L_bf, rhs=rhsH[:, hf * 512:(hf + 1) * 512],
                                                         start=True, stop=True),
                         reads=[R_rhs, R_cst], writes=[R_PB[3 + hf]], dur=0.3)
                seg = ps_t[:, 3:5, :].rearrange("p a b -> p (a b)")
                sch.op("act", lambda e: e.activation(out=dec, in_=seg, func=AF.Exp),
                     reads=[R_PB[3], R_PB[4]], writes=[R_dec], dur=1.05, tbl="E")
                xs_tm = PBbf[0][:, 0:512]
                for i in range(4):
                    sch.op("pe", lambda e, i=i: e.transpose(out=xs_tm[:, i * 128:(i + 1) * 128], in_=xsT[:, i, blk],
                                                          identity=ident_bf),
                         reads=[R_xsT, R_cst], writes=[R_PB[0]])
                bm_tm = PBbf[2][:, 512:640]
                sch.op("pe", lambda e: e.matmul(PB[2][:, 0:128], lhsT=bcT[:, 0, blk], rhs=bcT[:, 1, blk],
                                              start=True, stop=True), reads=[R_bcT], writes=[R_PB[2]])
                sch.op("pe", lambda e: e.transpose(out=bm_tm, in_=bcT[:, 0, blk], identity=ident_bf),
                     reads=[R_bcT, R_cst], writes=[R_PB[2]])
                sch.op("dve", lambda e: e.tensor_tensor(out=cbm, in0=PB[2][:, 0:128], in1=U_f, op=ALU.mult),
                     reads=[R_PB[2], R_cp], writes=[R_cbm])
                sch.op("act", lambda e: e.activation(out=bm_sb, in_=bm_tm, func=AF.Copy), reads=[R_PB[2]],
                       writes=[R_bmsb2[p2]], dur=0.3)
                sch.op("dve", lambda e: e.tensor_tensor(out=MT.rearrange("p (r l) -> p r l", r=8),
                                                      in0=dec.rearrange("p (r l) -> p r l", r=8),
                                                      in1=apx(cbm[:, 0:1], [[0, 8], [1, 128]]), op=ALU.mult),
                     reads=[R_dec, R_cbm], writes=[R_MT2[p2]], dur=1.14)
                sch.op("dve", lambda e: e.tensor_tensor(
                    out=xdt.rearrange("p (r q) -> p r q", r=8), in0=xs_tm.rearrange("p (r q) -> p r q", r=8),
                    in1=apx(dt_b[:, 0:1], [[1, 8], [0, 64]]), op=ALU.mult),
                     reads=[R_PB[0], R_dt], writes=[R_xdt2[p2]])
                sch.op("dve", lambda e: e.tensor_tensor(
                    out=xdd.rearrange("p (r q) -> p r q", r=8), in0=xdt.rearrange("p (r q) -> p r q", r=8),
                    in1=apx(dte[:, 0:1], [[1, 8], [0, 64]]), op=ALU.mult),
                     reads=[R_xdt2[p2], R_edc], writes=[R_xdd2[p2]], dur=0.5)
                szf, R_szf = sz_2[p2], R_sz_2[p2]
                for c in range(8):
                    sch.op("pe", lambda e, c=c: e.matmul(PB[7], lhsT=hT[:, c, blk], rhs=Wxz[:, c, :],
                                                       start=(c == 0), stop=(c == 7)),
                         reads=[R_hT, R_Wxz], writes=[R_PB[7]], dur=0.22)
                sch.op("act", lambda e: e.activation(out=szf, in_=PB[7], func=AF.Silu), reads=[R_PB[7]],
                       writes=[R_szf], dur=0.6, tbl="S")

            def back(b):
                blk = slice(b * 128, (b + 1) * 128)
                p2 = b % 2
                t1, R_t1 = t1_2[p2], R_t1_2[p2]
                sz, R_sz = sz_2[p2], R_sz_2[p2]
                MT, xdt, xdd, bm_sb = MT2[p2], xdt2[p2], xdd2[p2], bmsb2[p2]
                xs_tm = PBbf[0][:, 0:512]
                E_ = edc_all[:, b, g * 8:(g + 1) * 8]
                cd = edc_all[:, b, 32 + g * 8:40 + g * 8]
                for r in range(8):
                    sch.op("pe", lambda e, r=r: e.matmul(PB[5][:, r * 64:(r + 1) * 64],
                                                       lhsT=MT[:, r * 128:(r + 1) * 128],
                                                       rhs=xdt[:, r * 64:(r + 1) * 64], start=(r == 0), stop=False,
                                                       skip_group_check=True),
                         reads=[R_MT2[p2], R_xdt2[p2]], writes=[R_PB[5]], dur=0.06)
                for i in range(4):
                    sch.op("pe", lambda e, i=i: e.matmul(PB[5][:, i * 128:(i + 1) * 128], lhsT=xsT[:, i, blk],
                                                        rhs=diagD[i], start=False, stop=False,
                                                        skip_group_check=True),
                           reads=[R_xsT, R_dD], writes=[R_PB[5]], dur=0.11)
                sch.op("pe", lambda e: e.matmul(PB[6], lhsT=bcT[:, 1, blk], rhs=S_b, start=True, stop=True),
                     reads=[R_bcT, R_Sb], writes=[R_PB[6]])
                sch.op("dve", lambda e: e.tensor_tensor(
                    out=t1b.rearrange("p (r q) -> p r q", r=8), in0=PB[6].rearrange("p (r q) -> p r q", r=8),
                    in1=apx(E_[:, 0:1], [[1, 8], [0, 64]]), op=ALU.mult),
                     reads=[R_PB[6], R_edc], writes=[R_t1b])
                sch.op("pe", lambda e: e.matmul(PB[5], lhsT=ident_bf, rhs=t1b, start=False, stop=True,
                                                skip_group_check=True),
                       reads=[R_cst, R_t1b], writes=[R_PB[5]], dur=0.3)
                sch.op("dve", lambda e: e.tensor_tensor(out=t1, in0=PB[5], in1=sz, op=ALU.mult),
                     reads=[R_PB[5], R_sz], writes=[R_t1])
                ssq = smc[:, 32:33]
                vv = smc[:, 33:34]
                sch.op("act", lambda e: e.activation(out=sz, in_=t1, func=AF.Square, accum_out=ssq),
                     reads=[R_t1], writes=[R_sz, R_smc])
                sch.op("dve", lambda e: e.tensor_scalar(out=vv, in0=ssq, scalar1=1.0 / 512, scalar2=SSM_EPS,
                                                         op0=ALU.mult, op1=ALU.add), reads=[R_smc], writes=[R_smc],
                       dur=0.15)
                rsqrt_small(vv, vv, R_smc, em=sch)
                sch.op("act", lambda e: e.activation(out=o_tm, in_=t1, func=AF.Copy, scale=vv),
                       reads=[R_t1, R_smc], writes=[R_otm], dur=0.65)
                ot_ps = PBbf[1][:, 0:512].rearrange("p (c t) -> p c t", c=4)
                for i in range(4):
                    sch.op("pe", lambda e, i=i: e.transpose(out=ot_ps[:, i, :], in_=o_tm[:, i * 128:(i + 1) * 128],
                                                          identity=ident_bf),
                         reads=[R_otm, R_cst], writes=[R_PB[1]])
                sch.op("act", lambda e: e.activation(out=mixT[:, 8 + g * 4:12 + g * 4, b * 128:(b + 1) * 128],
                                                   in_=ot_ps, func=AF.Copy),
                     reads=[R_PB[1]], writes=[R_mix])
                sch.op("pe", lambda e: e.matmul(PB[6], lhsT=bm_sb, rhs=xdd, start=True, stop=True),
                     reads=[R_bmsb2[p2], R_xdd2[p2]], writes=[R_PB[6]])
                sch.op("dve", lambda e: e.tensor_tensor(
                    out=S_f.rearrange("p (r q) -> p r q", r=8), in0=S_f.rearrange("p (r q) -> p r q", r=8),
                    in1=apx(cd[:, 0:1], [[1, 8], [0, 64]]), op=ALU.mult), reads=[R_S, R_edc], writes=[R_S])
                sch.op("dve", lambda e: e.tensor_tensor(out=S_f, in0=PB[6], in1=S_f, op=ALU.add),
                     reads=[R_PB[6], R_S], writes=[R_S])
                sch.op("act", lambda e: e.activation(out=S_b, in_=S_f, func=AF.Copy), reads=[R_S], writes=[R_Sb],
                       dur=0.65)

            for b in range(NB):
                front(b)
                back(b)
            est = sch.flush()
            print('[sched] ssd group', g, 'estimated us', round(est, 1))
            if g == 0:
                k.barrier([])

        if dbg in ("C", "C1"):
            k.barrier(dma_res)
            k.op("dve", lambda e: e.tensor_copy(out=work[:, 0:16384], in_=big[:, 16384:32768]),
                 reads=[R_mix], writes=[R_dbg])
            k.dma("sp", dbg_out[:, 0:16384], work[:, 0:16384], reads=[R_dbg])
            k.eng["sp"].wait_ge(k.dsems[R_dbg.dsem], R_dbg.dcnt)
            return nc

        k.barrier(dma_res)
        cv = Carver(work)
        wo1 = cv.bf16(8192).rearrange("p (c n) -> p c n", c=16)
        R_wo1 = wres("wo1")
        k.dma("pool", wo1, w_out[:, 512:1024].rearrange("(c p) n -> p c n", p=128), writes=[R_wo1])
        for wo, R_wo in ((wo0, R_big2), (wo1, R_wo1)):
            for c in range(8, 16):
                k.op("dve", lambda e, wo=wo, c=c: e.tensor_scalar(
                    out=wo[:, c, :], in0=wo[:, c, :], scalar1=cp[:, C_NGCOL + c - 8:C_NGCOL + c - 7], scalar2=None,
                    op0=ALU.mult), reads=[R_wo, R_cp], writes=[R_wo])
        XB = [cv.f32(1024) for _ in range(2)]
        R_XB = [wres("dxb%d" % i) for i in range(2)]
        X2 = [cv.f32(1024) for _ in range(2)]
        R_X2 = [wres("dx2%d" % i) for i in range(2)]
        D_sets = []
        for i in range(2):
            s = cv.f32(4)
            D_sets.append((s[:, 0:1], s[:, 1:2], s[:, 2:3], cv.bf16(1024), cv.bf16(1024), Res("d_s%d" % i),
                           Res("d_j%d" % i), Res("d_hb%d" % i)))
        def d_mm(b):
            xb, R_xb = XB[b % 2], R_XB[b % 2]
            x2, R_x2 = X2[b % 2], R_X2[b % 2]
            k.dma("sp", xb, x[b * 128:(b + 1) * 128, :], writes=[R_xb])
            for half, (wo, R_wo) in enumerate(((wo0, R_big2), (wo1, R_wo1))):
                pbi = 2 * (b % 2) + half
                for c in range(16):
                    k.op("pe", lambda e, c=c, pbi=pbi, wo=wo, b=b: e.matmul(
                        PB[pbi], lhsT=mixT[:, c, b * 128:(b + 1) * 128], rhs=wo[:, c, :], start=(c == 0),
                        stop=(c == 15)), reads=[R_mix, R_wo], writes=[R_PB[pbi]])
                k.op("dve", lambda e, half=half, pbi=pbi, xb=xb, x2=x2: e.tensor_tensor(
                    out=x2[:, half * 512:(half + 1) * 512], in0=PB[pbi], in1=xb[:, half * 512:(half + 1) * 512],
                    op=ALU.add), reads=[R_PB[pbi], R_xb], writes=[R_x2])

        def d_pre(b):
            x2, R_x2 = X2[b % 2], R_X2[b % 2]
            cvs = D_sets[b % 2]
            ss, v, rstd, junk, hb, R_s, R_j, R_hb = cvs
            k.dma("pool", x2d[b * 128:(b + 1) * 128, :], x2, reads=[R_x2], writes=[R_x2d[b]])
            norm_stats(x2, R_x2, cvs)
            rsqrt_small(v, rstd, R_s)
            k.op("dve", lambda e: e.tensor_scalar(out=hb, in0=x2, scalar1=rstd, scalar2=None, op0=ALU.mult),
                 reads=[R_x2, R_s], writes=[R_hb])

        def d_pe(b):
            ss, v, rstd, junk, hb, R_s, R_j, R_hb = D_sets[b % 2]
            pbi = 4 + b % 2
            pv = PBbf[pbi].rearrange("p (c t) -> p c t", c=8)
            for c in range(8):
                k.op("pe", lambda e, c=c: e.transpose(out=pv[:, c, :], in_=hb[:, c * 128:(c + 1) * 128],
                                                      identity=ident_bf),
                     reads=[R_hb, R_cst], writes=[R_PB[pbi]])
            gb = apx(cp[:, C_GF:C_GF + 1], [[1, 8], [0, 128]])
            k.op("dve", lambda e: e.tensor_tensor(out=hT[:, :, b * 128:(b + 1) * 128], in0=pv, in1=gb,
                                                  op=ALU.mult),
                 reads=[R_PB[pbi], R_cp], writes=[R_hT])

        d_mm(0)
        d_pre(0)
        for b in range(NB):
            if b + 1 < NB:
                d_mm(b + 1)
                d_pre(b + 1)
            d_pe(b)

        if dbg == "D":
            k.barrier(dma_res + R_x2d)
            k.op("dve", lambda e: e.tensor_copy(out=work[:, 0:16384], in_=hT.rearrange("p c t -> p (c t)")),
                 reads=[R_hT], writes=[R_dbg])
            k.dma("sp", dbg_out[:, 0:16384], work[:, 0:16384], reads=[R_dbg])
            k.eng["sp"].wait_ge(k.dsems[R_dbg.dsem], R_dbg.dcnt)
            return nc

        k.barrier(dma_res + R_x2d)
        actT = big[:, 0:22528].rearrange("p (c t) -> p c t", c=22)
        wd = big[:, 22528:45056].rearrange("p (c n) -> p c n", c=22)
        R_actT = Res("actT")
        R_wd = Res("wd")
        cv = Carver(work)
        NWB = 3
        WG = [cv.bf16(2048).rearrange("p (c n) -> p c n", c=8) for _ in range(NWB)]
        WV = [cv.bf16(2048).rearrange("p (c n) -> p c n", c=8) for _ in range(NWB)]
        R_WG = [wres("wg%d" % i) for i in range(NWB)]
        R_WV = [wres("wv%d" % i) for i in range(NWB)]

        def load_up(seq):
            j = seq % 11
            load_w(WG[seq % NWB], w_up[:, 256 * j:256 * (j + 1)], R_WG[seq % NWB])
            load_w(WV[seq % NWB], w_up[:, FFN + 256 * j:FFN + 256 * (j + 1)], R_WV[seq % NWB])

        load_up(0)
        load_up(1)
        ACC = [cv.f32(1028) for _ in range(2)]
        R_ACC = [Res("acc%d" % i) for i in range(2)]
        SG = cv.f32(1024)
        R_SG = Res("sg")
        FX = [cv.f32(1024) for _ in range(2)]
        R_FX = [wres("fx%d" % i) for i in range(2)]
        FY = [cv.f32(1024) for _ in range(2)]
        R_FY = [wres("fy%d" % i) for i in range(2)]
        fj = cv.bf16(1024)
        R_fj = Res("fj")
        fs = cv.f32(8)
        R_fs = Res("fs")

        ch_i = 0
        oblk = 0
        for hf in range(2):
            halo = 2 if hf == 1 else 0
            t0 = hf * 1024 - halo
            ncol = 1024 + halo
            tiles = []
            c0 = 0
            while c0 < ncol:
                n = min(512, ncol - c0)
                tiles.append((c0, n))
                c0 += n
            for j in range(11):
                seq = hf * 11 + j
                wg, R_wg = WG[seq % NWB], R_WG[seq % NWB]
                wv, R_wv = WV[seq % NWB], R_WV[seq % NWB]
                if seq + 2 < 22:
                    load_up(seq + 2)
                if seq == 1:
                    for i2 in range(2):
                        k.dma("pool", wd[:, 11 * i2:11 * (i2 + 1), :],
                              w_down[1408 * i2:1408 * (i2 + 1), :].rearrange("(c p) n -> p c n", p=128),
                              writes=[R_wd])
                for sub in range(2):
                    i = 2 * j + sub
                    ctxs = []
                    for kind, (Wt, R_W) in enumerate(((wg, R_wg), (wv, R_wv))):
                        nbk = len(tiles)
                        bset = nbk * (ch_i % (6 // nbk))
                        ch_i += 1
                        u = ps_t[:, bset:bset + nbk, :].rearrange("p a b -> p (a b)")
                        R_u = [R_PB[bset + q] for q in range(nbk)]
                        chunk_id = i + 22 * kind
                        cw = cp[:, C_CWF + 3 * chunk_id:C_CWF + 3 * chunk_id + 3]
                        cb = cp[:, C_CBF + chunk_id:C_CBF + chunk_id + 1]
                        ctxs.append((Wt, R_W, bset, u, R_u, cw, cb, ACC[kind], R_ACC[kind]))

                    def e_mm(kind):
                        Wt, R_W, bset, u, R_u, cw, cb, acc, R_acc = ctxs[kind]
                        for ti, (c0, n) in enumerate(tiles):
                            for c in range(8):
                                k.op("pe", lambda e, c=c, ti=ti, c0=c0, n=n: e.matmul(
                                    PB[bset + ti][:, 0:n], lhsT=Wt[:, c, sub * 128:(sub + 1) * 128],
                                    rhs=hT[:, c, t0 + c0:t0 + c0 + n], start=(c == 0), stop=(c == 7)),
                                     reads=[R_W, R_hT], writes=[R_PB[bset + ti]])

                    def e_id(kind):
                        Wt, R_W, bset, u, R_u, cw, cb, acc, R_acc = ctxs[kind]
                        k.op("act", lambda e: e.activation(out=acc[:, 0:ncol], in_=u[:, 0:ncol], func=AF.Identity,
                                                           scale=cw[:, 2:3], bias=cb),
                             reads=R_u + [R_cp], writes=[R_acc])

                    def e_sh(kind):
                        Wt, R_W, bset, u, R_u, cw, cb, acc, R_acc = ctxs[kind]
                        for sh in (1, 2):
                            k.op("dve", lambda e, sh=sh: e.scalar_tensor_tensor(
                                out=acc[:, sh:ncol], in0=u[:, 0:ncol - sh], scalar=cw[:, 2 - sh:3 - sh],
                                in1=acc[:, sh:ncol], op0=ALU.mult, op1=ALU.add),
                                 reads=R_u + [R_cp, R_acc], writes=[R_acc])

                    e_mm(0)
                    e_id(0)
                    e_sh(0)
                    e_mm(1)
                    e_id(1)
                    accg, accv_ = ctxs[0][7], ctxs[1][7]
                    k.op("act", lambda e: e.activation(out=SG, in_=accg[:, halo:halo + 1024], func=AF.Silu),
                         reads=[R_ACC[0]], writes=[R_SG])
                    e_sh(1)
                    k.op("dve", lambda e, i=i: e.tensor_tensor(out=actT[:, i, :], in0=accv_[:, halo:halo + 1024],
                                                               in1=SG, op=ALU.mult),
                         reads=[R_ACC[1], R_SG], writes=[R_actT])
            for bl in range(8):
                b = hf * 8 + bl
                fx, R_fx = FX[b % 2], R_FX[b % 2]
                fy, R_fy = FY[b % 2], R_FY[b % 2]
                k.dma("sp", fx, x2d[b * 128:(b + 1) * 128, :], reads=[R_x2d[b]], writes=[R_fx])
                for half in range(2):
                    pbi = 2 * (b % 4) + half
                    for i in range(22):
                        k.op("pe", lambda e, i=i, half=half, pbi=pbi, bl=bl: e.matmul(
                            PB[pbi], lhsT=actT[:, i, bl * 128:(bl + 1) * 128],
                            rhs=wd[:, i, half * 512:(half + 1) * 512], start=(i == 0), stop=(i == 21)),
                             reads=[R_actT, R_wd], writes=[R_PB[pbi]])
                    k.op("dve", lambda e, half=half, pbi=pbi, fx=fx: e.tensor_tensor(
                        out=fx[:, half * 512:(half + 1) * 512], in0=PB[pbi], in1=fx[:, half * 512:(half + 1) * 512],
                        op=ALU.add), reads=[R_PB[pbi], R_fx], writes=[R_fx])
                k.op("act", lambda e, fx=fx: e.activation(out=fj, in_=fx, func=AF.Square, accum_out=fs[:, 0:1]),
                     reads=[R_fx], writes=[R_fj, R_fs])
                k.op("dve", lambda e: e.tensor_scalar(out=fs[:, 1:2], in0=fs[:, 0:1], scalar1=1.0 / D, scalar2=EPS,
                                                       op0=ALU.mult, op1=ALU.add), reads=[R_fs], writes=[R_fs])
                rsqrt_small(fs[:, 1:2], fs[:, 1:2], R_fs)
                k.op("dve", lambda e, fx=fx, fy=fy: e.scalar_tensor_tensor(
                    out=fy, in0=fx, scalar=fs[:, 1:2], in1=cp[:, C_FG:C_FG + 1024], op0=ALU.mult, op1=ALU.mult),
                     reads=[R_fx, R_fs, R_cp], writes=[R_fy])
                k.dma("pool", out[b * 128:(b + 1) * 128, :], fy, reads=[R_fy], writes=[R_out[oblk % 4]])
                oblk += 1

        for r in R_out:
            if r.dsem is not None:
                k.eng["sp"].wait_ge(k.dsems[r.dsem], r.dcnt)
        for r in R_FY:
            if r.dsem is not None:
                k.eng["sp"].wait_ge(k.dsems[r.dsem], r.dcnt)
    return nc


def _t5_bucket(rel):
    nb = 16
    max_exact = 8
    bucket = np.where(rel > 0, nb, 0)
    n = np.abs(rel)
    nf = np.maximum(n, 1).astype(np.float32)
    large = max_exact + (np.log(nf / np.float32(max_exact)) / np.float32(math.log(128 / max_exact))
                         * np.float32(nb - max_exact)).astype(np.int32)
    large = np.minimum(large, nb - 1)
    return bucket + np.where(n < max_exact, n, large)


def _host_pack(inp):
    f = np.float32
    cp = np.zeros((128, NCP), f)
    cp[:, C_GA:C_GA + 8] = inp["attn_norm_g"][0].reshape(8, 128).T
    cp[:, C_GF:C_GF + 8] = inp["ffn_norm_g"][0].reshape(8, 128).T
    cp[:, C_SUBG] = inp["attn_subln_g"][0]
    for i, nm in enumerate(("lambda_q1", "lambda_k1", "lambda_q2", "lambda_k2")):
        cp[:, C_LAM + 64 * i:C_LAM + 64 * (i + 1)] = inp[nm][0][None, :]
    cp[:, C_CWS:C_CWS + 48] = inp["ssm_conv_w"][0].reshape(4, 12, 128).transpose(2, 1, 0).reshape(128, 48)
    cp[:, C_CBS:C_CBS + 12] = inp["ssm_conv_b"][0].reshape(12, 128).T
    cp[:, C_CWF:C_CWF + 132] = inp["ffn_conv_w"][0].reshape(3, 44, 128).transpose(2, 1, 0).reshape(128, 132)
    cp[:, C_CBF:C_CBF + 44] = inp["ffn_conv_b"][0].reshape(44, 128).T
    cp[:, C_DTB:C_DTB + 16] = inp["ssm_dt_bias"][0][None, :]
    cp[:, C_ALOG:C_ALOG + 16] = inp["ssm_a_log"][0][None, :]
    cp[:, C_D:C_D + 16] = inp["ssm_d"][0][None, :]
    tab = inp["rel_bias_table"]
    cp[:, C_FAR:C_FAR + 8] = tab[15][None, :]
    cp[:, C_FG:C_FG + 1024] = inp["final_norm_g"][None, :]
    cp[:, C_NG:C_NG + 1024] = inp["ssm_norm_g"][0][None, :]
    idx = np.arange(128)
    cp[:, C_ID:C_ID + 128] = np.eye(128, dtype=f)
    cp[:, C_U:C_U + 128] = (idx[:, None] <= idx[None, :]).astype(f)
    cp[:, C_L:C_L + 128] = (idx[:, None] > idx[None, :]).astype(f)
    cp[:, C_ONE:C_ONE + 128] = 1.0
    cp[:, C_DCOL:C_DCOL + 8] = inp["ssm_d"][0][(np.arange(8)[None, :] * 128 + idx[:, None]) // 64]
    cp[:, C_NGCOL:C_NGCOL + 8] = inp["ssm_norm_g"][0].reshape(8, 128).T
    kk = idx[:, None]
    qq = np.arange(384)[None, :]
    rel = kk - qq
    bidx = _t5_bucket(rel)
    bt = tab[bidx]
    bt = np.ascontiguousarray(bt.transpose(0, 2, 1)).astype(f)
    masked = (kk >= 64) & (qq < 64)
    bt[np.broadcast_to(masked[:, None, :], bt.shape)] = -30000.0
    return cp, bt.reshape(128, 8 * 384)


_NC_CACHE = {}


def kernel(**inputs):
    inp = {k_: np.asarray(v) for k_, v in inputs.items()}
    cp, bt = _host_pack(inp)
    if "nc" not in _NC_CACHE:
        _NC_CACHE["nc"] = build_program()
    nc = _NC_CACHE["nc"]
    x = np.ascontiguousarray(inp["x"], dtype=np.float32)
    shared = {
        "w_in": np.ascontiguousarray(inp["w_in"][0]),
        "w_out": np.ascontiguousarray(inp["w_out"][0]),
        "w_up": np.ascontiguousarray(inp["ffn_w_up"][0]),
        "w_down": np.ascontiguousarray(inp["ffn_w_down"][0]),
        "cpack": cp,
        "biasT": bt,
    }
    in_maps = [dict(shared, x=x[i]) for i in range(8)]
    res = run_bass_kernel_spmd(nc, in_maps, core_ids=list(range(8)))
    return np.stack([np.asarray(r["out"], dtype=np.float32) for r in res.results], axis=0)
```
